# Optimizing a Trainium2 kernel written in Bass

```python
import math
import jax, jax.numpy as jnp
from jax import lax
import numpy as np

D_MODEL = 1024
BATCH = 8
SEQ = 2048
DEPTH = 2

HEAD_DIM = 64
A_HEADS = 4
A_KV_HEADS = 2
B_HEADS = 6
C_HEADS = 6
D_MIX = (A_HEADS + B_HEADS + C_HEADS) * HEAD_DIM
D_FF = 2816
GRID_W = 64
ROPE_THETA = 10000.0
Q_BLOCK = 128
DIL_PAIRS = ((128, 1), (512, 4), (2048, 16))
REL_BUCKETS = 32
REL_MAX_DIST = 1024
CONV_K = 5
CHUNK = 64
NORM_EPS = 1e-6
NEG_INF = -1e30

A_QW = A_HEADS * HEAD_DIM
A_KVW = A_KV_HEADS * HEAD_DIM
B_W = B_HEADS * HEAD_DIM
C_W = C_HEADS * HEAD_DIM
IN_SIZES = (A_QW, A_KVW, A_KVW, B_W, B_W, B_W, C_W, C_W, C_W, C_W, 2 * C_HEADS, 2 * C_HEADS)
N_IN = sum(IN_SIZES)
IN_SPLITS = [int(s) for s in np.cumsum(IN_SIZES)[:-1]]

kernel_name = 'hymba_style_hybrid_encoder'

F32 = jnp.float32


def rms_norm(x, g):
    xf = x.astype(F32)
    y = xf * lax.rsqrt(jnp.mean(xf * xf, axis=-1, keepdims=True) + NORM_EPS)
    return (y * g.astype(F32)).astype(x.dtype)


def l2norm(x):
    return x * lax.rsqrt(jnp.sum(x * x, axis=-1, keepdims=True) + NORM_EPS)


def swiglu(x, wg, wu, wd):
    return (jax.nn.silu(x @ wg) * (x @ wu)) @ wd


def axial_rope(seq):
    rows = seq // GRID_W
    row = jnp.repeat(jnp.arange(rows), GRID_W).astype(F32)
    col = jnp.tile(jnp.arange(GRID_W), rows).astype(F32)
    n_freq = HEAD_DIM // 4
    inv = ROPE_THETA ** (-jnp.arange(n_freq, dtype=F32) / n_freq)
    ang = jnp.concatenate([row[:, None] * inv, col[:, None] * inv], axis=-1)
    return jnp.cos(ang), jnp.sin(ang)


def apply_rope(x, cos, sin):
    xf = x.astype(F32)
    half = HEAD_DIM // 2
    x1, x2 = xf[..., :half], xf[..., half:]
    c, s = cos[None, :, None, :], sin[None, :, None, :]
    return jnp.concatenate([x1 * c - x2 * s, x1 * s + x2 * c], axis=-1).astype(x.dtype)


def dense_gqa(q, k, v):
    B, S, Hq, D = q.shape
    Hkv = k.shape[2]
    G = Hq // Hkv
    nb = S // Q_BLOCK
    qb = q.reshape(B, nb, Q_BLOCK, Hkv, G, D).transpose(1, 0, 2, 3, 4, 5)

    def block(qi):
        s = jnp.einsum('bqhgd,bkhd->bhgqk', qi, k).astype(F32) * (D ** -0.5)
        p = jax.nn.softmax(s, axis=-1)
        return jnp.einsum('bhgqk,bkhd->bqhgd', p.astype(v.dtype), v)

    o = lax.map(block, qb)
    return o.transpose(1, 0, 2, 3, 4, 5).reshape(B, S, Hq * D)


def t5_bucket(rel):
    half = REL_BUCKETS // 2
    exact = half // 2
    sign = jnp.where(rel > 0, half, 0)
    n = jnp.abs(rel)
    nf = jnp.maximum(n, 1).astype(F32)
    large = exact + (jnp.log(nf / exact) / math.log(REL_MAX_DIST / exact) * (half - exact)).astype(jnp.int32)
    large = jnp.minimum(large, half - 1)
    return sign + jnp.where(n < exact, n, large)


def dilated_branch(q, k, v, rel_bias, window, dil):
    B, S, H, D = q.shape
    side = window // (2 * dil)
    blk = side
    L = S // dil
    nb = -(-L // blk)
    Lp = nb * blk

    def to_sub(x):
        x = x.reshape(B, L, dil, H, D).transpose(0, 2, 1, 3, 4)
        return jnp.pad(x, ((0, 0), (0, 0), (0, Lp - L), (0, 0), (0, 0)))

    def windows(x):
        xb = to_sub(x).reshape(B, dil, nb, blk, H, D)
        xb = jnp.pad(xb, ((0, 0), (0, 0), (1, 1), (0, 0), (0, 0), (0, 0)))
        return jnp.concatenate([xb[:, :, :-2], xb[:, :, 1:-1], xb[:, :, 2:]], axis=3)

    qs = to_sub(q).reshape(B, dil, nb, blk, H, D)
    kw, vw = windows(k), windows(v)
    s = jnp.einsum('brnqhd,brnkhd->brnhqk', qs, kw).astype(F32) * (D ** -0.5)
    qi = jnp.arange(blk)
    kj = jnp.arange(3 * blk)
    delta = kj[None, :] - blk - qi[:, None]
    key_sub = jnp.arange(nb)[:, None] * blk - blk + kj[None, :]
    mask = (jnp.abs(delta) <= side)[None] & ((key_sub >= 0) & (key_sub < L))[:, None, :]
    bias = rel_bias[t5_bucket(delta * dil)].astype(F32).transpose(2, 0, 1)
    s = jnp.where(mask[:, None], s + bias, NEG_INF)
    lse = jax.nn.logsumexp(s, axis=-1)
    p = jnp.exp(s - lse[..., None])
    o = jnp.einsum('brnhqk,brnkhd->brnqhd', p.astype(v.dtype), vw)
    o = o.reshape(B, dil, Lp, H, D)[:, :, :L].transpose(0, 2, 1, 3, 4).reshape(B, S, H, D)
    lse = lse.transpose(0, 1, 2, 4, 3).reshape(B, dil, Lp, H)[:, :, :L].transpose(0, 2, 1, 3).reshape(B, S, H)
    return o, lse


def dilated_mixture(q, k, v, rel_bias):
    outs, lses = [], []
    for window, dil in DIL_PAIRS:
        o, l = dilated_branch(q, k, v, rel_bias, window, dil)
        outs.append(o)
        lses.append(l)
    wts = jax.nn.softmax(jnp.stack(lses, axis=-1), axis=-1)
    o = jnp.einsum('bshgd,bshg->bshd', jnp.stack(outs, axis=3), wts.astype(outs[0].dtype))
    B, S, H, D = q.shape
    return o.reshape(B, S, H * D)


def short_conv(x, w):
    K, C = w.shape
    return lax.conv_general_dilated(x, w[:, None, :].astype(x.dtype), window_strides=(1,),
                                    padding=[(K // 2, K // 2)],
                                    dimension_numbers=('NWC', 'WIO', 'NWC'),
                                    feature_group_count=C)


def gated_delta_chunked(q, k, v, g, beta):
    B, H, S, Dk = q.shape
    Dv = v.shape[-1]
    N = S // CHUNK
    qc = q.reshape(B, H, N, CHUNK, Dk)
    kc = k.reshape(B, H, N, CHUNK, Dk)
    vc = v.reshape(B, H, N, CHUNK, Dv)
    gc = jnp.cumsum(g.reshape(B, H, N, CHUNK), axis=-1)
    bc = beta.reshape(B, H, N, CHUNK)
    tril = jnp.tril(jnp.ones((CHUNK, CHUNK), dtype=bool))
    strict = jnp.tril(jnp.ones((CHUNK, CHUNK), dtype=bool), -1)
    diff = gc[..., :, None] - gc[..., None, :]
    decay = jnp.where(tril, jnp.exp(jnp.where(tril, diff, 0.0)), 0.0)
    kb = kc * bc[..., None]
    vb = vc * bc[..., None]
    M = jnp.where(strict, jnp.einsum('bhnid,bhnjd->bhnij', kb, kc) * decay, 0.0)
    eye = jnp.eye(CHUNK, dtype=F32)
    T = lax.linalg.triangular_solve(eye + M, jnp.broadcast_to(eye, M.shape), left_side=True,
                                    lower=True, unit_diagonal=True)
    u = jnp.einsum('bhnij,bhnjv->bhniv', T, vb)
    w = jnp.einsum('bhnij,bhnjk->bhnik', T, kb * jnp.exp(gc)[..., None])
    a_intra = jnp.einsum('bhnid,bhnjd->bhnij', qc, kc) * decay

    def step(state, xs):
        q_i, k_i, u_i, w_i, g_i, a_i = xs
        v_new = u_i - jnp.einsum('bhck,bhkv->bhcv', w_i, state)
        o = (jnp.einsum('bhck,bhkv->bhcv', q_i * jnp.exp(g_i)[..., None], state)
             + jnp.einsum('bhij,bhjv->bhiv', a_i, v_new))
        g_last = g_i[..., -1:]
        state = (state * jnp.exp(g_last)[..., None]
                 + jnp.einsum('bhck,bhcv->bhkv', k_i * jnp.exp(g_last - g_i)[..., None], v_new))
        return state, o

    xs = tuple(jnp.moveaxis(t, 2, 0) for t in (qc, kc, u, w, gc, a_intra))
    state0 = jnp.zeros((B, H, Dk, Dv), F32)
    _, o = lax.scan(step, state0, xs)
    return jnp.moveaxis(o, 0, 2).reshape(B, H, S, Dv)


def gated_deltanet_bidir(cq, ck, cv, cz, cb, ca, conv_w, A_log, dt_bias, out_gain):
    B, S, _ = cq.shape
    qkv = jax.nn.silu(short_conv(jnp.concatenate([cq, ck, cv], axis=-1), conv_w)).astype(F32)
    q, k, v = jnp.split(qkv, 3, axis=-1)

    def heads(t):
        return t.reshape(B, S, C_HEADS, HEAD_DIM).transpose(0, 2, 1, 3)

    q = l2norm(heads(q)) * (HEAD_DIM ** -0.5)
    k = l2norm(heads(k))
    v = heads(v)
    beta = jax.nn.sigmoid(cb.astype(F32)).reshape(B, S, 2, C_HEADS).transpose(2, 0, 3, 1)
    g = (-jnp.exp(A_log.astype(F32))
         * jax.nn.softplus(ca.astype(F32).reshape(B, S, 2, C_HEADS) + dt_bias.astype(F32)))
    g = g.transpose(2, 0, 3, 1)
    flip = lambda t: jnp.flip(t, axis=2)
    qq = jnp.concatenate([q, flip(q)], axis=0)
    kk = jnp.concatenate([k, flip(k)], axis=0)
    vv = jnp.concatenate([v, flip(v)], axis=0)
    gg = jnp.concatenate([g[0], flip(g[1])], axis=0)
    bb = jnp.concatenate([beta[0], flip(beta[1])], axis=0)
    o = gated_delta_chunked(qq, kk, vv, gg, bb)
    o = o[:B] + flip(o[B:])
    o = rms_norm(o, out_gain) * jax.nn.silu(heads(cz.astype(F32)))
    return o.transpose(0, 2, 1, 3).reshape(B, S, C_W).astype(cq.dtype)


def hybrid_mixer(h, rel_bias, w_in, a_qn, a_kn, b_qn, b_kn, c_conv, c_A_log, c_dt_bias, c_out_norm, w_out):
    B, S, _ = h.shape
    proj = h @ w_in
    aq, ak, av, bq, bk, bv, cq, ck, cv, cz, cb, ca = jnp.split(proj, IN_SPLITS, axis=-1)
    cos, sin = axial_rope(S)
    aq = apply_rope(rms_norm(aq.reshape(B, S, A_HEADS, HEAD_DIM), a_qn), cos, sin)
    ak = apply_rope(rms_norm(ak.reshape(B, S, A_KV_HEADS, HEAD_DIM), a_kn), cos, sin)
    av = av.reshape(B, S, A_KV_HEADS, HEAD_DIM)
    out_a = dense_gqa(aq, ak, av)
    bq = rms_norm(bq.reshape(B, S, B_HEADS, HEAD_DIM), b_qn)
    bk = rms_norm(bk.reshape(B, S, B_HEADS, HEAD_DIM), b_kn)
    bv = bv.reshape(B, S, B_HEADS, HEAD_DIM)
    out_b = dilated_mixture(bq, bk, bv, rel_bias).astype(h.dtype)
    out_c = gated_deltanet_bidir(cq, ck, cv, cz, cb, ca, c_conv, c_A_log, c_dt_bias, c_out_norm)
    return jnp.concatenate([out_a, out_b, out_c], axis=-1) @ w_out


def setup_inputs(seed: int = 0) -> dict:
    key = jax.random.key(seed)
    ks = iter(jax.random.split(key, 40))
    L = DEPTH

    def nrm(shape, scale):
        return jax.random.normal(next(ks), shape, F32) * scale

    def gain(shape):
        return 1.0 + 0.02 * jax.random.normal(next(ks), shape, F32)

    x = nrm((BATCH, SEQ, D_MODEL), 1.0)
    rel_bias = nrm((REL_BUCKETS, B_HEADS), 0.5)
    ffn1_norm = gain((L, D_MODEL))
    ffn1_w_gate = nrm((L, D_MODEL, D_FF), D_MODEL ** -0.5)
    ffn1_w_up = nrm((L, D_MODEL, D_FF), D_MODEL ** -0.5)
    ffn1_w_down = nrm((L, D_FF, D_MODEL), D_FF ** -0.5)
    mix_norm = gain((L, D_MODEL))
    w_in = nrm((L, D_MODEL, N_IN), D_MODEL ** -0.5)
    a_q_norm = gain((L, HEAD_DIM))
    a_k_norm = gain((L, HEAD_DIM))
    b_q_norm = gain((L, HEAD_DIM))
    b_k_norm = gain((L, HEAD_DIM))
    c_conv = nrm((L, CONV_K, 3 * C_W), CONV_K ** -0.5)
    c_A_log = jnp.log(jax.random.uniform(next(ks), (L, 2, C_HEADS), F32, 1.0, 16.0))
    dt = jnp.exp(jax.random.uniform(next(ks), (L, 2, C_HEADS), F32, math.log(1e-3), math.log(1e-1)))
    c_dt_bias = dt + jnp.log(-jnp.expm1(-dt))
    c_out_norm = gain((L, HEAD_DIM))
    w_out = nrm((L, D_MIX, D_MODEL), D_MIX ** -0.5)
    ffn2_norm = gain((L, D_MODEL))
    ffn2_w_gate = nrm((L, D_MODEL, D_FF), D_MODEL ** -0.5)
    ffn2_w_up = nrm((L, D_MODEL, D_FF), D_MODEL ** -0.5)
    ffn2_w_down = nrm((L, D_FF, D_MODEL), D_FF ** -0.5)
    return {'x': x, 'rel_bias': rel_bias,
            'ffn1_norm': ffn1_norm, 'ffn1_w_gate': ffn1_w_gate, 'ffn1_w_up': ffn1_w_up, 'ffn1_w_down': ffn1_w_down,
            'mix_norm': mix_norm, 'w_in': w_in,
            'a_q_norm': a_q_norm, 'a_k_norm': a_k_norm, 'b_q_norm': b_q_norm, 'b_k_norm': b_k_norm,
            'c_conv': c_conv, 'c_A_log': c_A_log, 'c_dt_bias': c_dt_bias, 'c_out_norm': c_out_norm,
            'w_out': w_out,
            'ffn2_norm': ffn2_norm, 'ffn2_w_gate': ffn2_w_gate, 'ffn2_w_up': ffn2_w_up, 'ffn2_w_down': ffn2_w_down}


def reference(x, rel_bias, ffn1_norm, ffn1_w_gate, ffn1_w_up, ffn1_w_down, mix_norm, w_in,
              a_q_norm, a_k_norm, b_q_norm, b_k_norm, c_conv, c_A_log, c_dt_bias, c_out_norm,
              w_out, ffn2_norm, ffn2_w_gate, ffn2_w_up, ffn2_w_down):
    h = x
    for l in range(DEPTH):
        h = h + 0.5 * swiglu(rms_norm(h, ffn1_norm[l]), ffn1_w_gate[l], ffn1_w_up[l], ffn1_w_down[l])
        h = h + hybrid_mixer(rms_norm(h, mix_norm[l]), rel_bias, w_in[l], a_q_norm[l], a_k_norm[l],
                             b_q_norm[l], b_k_norm[l], c_conv[l], c_A_log[l], c_dt_bias[l],
                             c_out_norm[l], w_out[l])
        h = h + 0.5 * swiglu(rms_norm(h, ffn2_norm[l]), ffn2_w_gate[l], ffn2_w_up[l], ffn2_w_down[l])
    return h
```

```python
import contextlib
import math
import numpy as np
import concourse.bass as bass
import concourse.mybir as mybir
from concourse.bass_utils import run_bass_kernel_spmd

F32 = mybir.dt.float32
BF16 = mybir.dt.bfloat16
AF = mybir.ActivationFunctionType
ALU = mybir.AluOpType
AX = mybir.AxisListType

S = 2048
D = 1024
DFF = 2816
NL = 2
NF = DFF // 128
EPS = 1e-6
GROUPS = ((0, 8), (8, 16), (16, 22))


class Sem:
    def __init__(self, h, name):
        self.h = h
        self.cnt = 0
        self.name = name


class Eng:
    def __init__(self, name, h, sem, selfsync):
        self.name = name
        self.h = h
        self.sem = sem
        self.selfsync = selfsync
        self.known = {}


def _box(ap):
    t = ap.tensor
    name = t.name
    space = str(ap.space)
    if "SB" not in space and "PSUM" not in space:
        return (name, 0, 1, 0, 1)
    dims = ap.ap
    shape = t.shape
    row = 1
    for s in list(shape)[1:]:
        row *= int(s)
    off = int(ap.offset)
    p0 = off // row
    f0 = off % row
    pstep, pcnt = dims[0]
    ext = 0
    for st, cn in dims[1:]:
        ext += (cn - 1) * abs(st)
    if pstep == 0:
        pcnt = 1
    sz = mybir.dt.size(ap.dtype)
    if "PSUM" in space:
        return (name, 0, 128, 0, 2048)
    return (name, p0, p0 + pcnt, f0 * sz, (f0 + ext + 1) * sz)


BK = 2048


def _bks(b):
    return range(b[3] // BK, (b[4] - 1) // BK + 1)


class KB:
    def __init__(self, nc, es):
        self.nc = nc
        self.es = es
        self.recs = {}
        self.notrack = set()
        self.pe = Eng("pe", nc.tensor, self.newsem("s_pe"), False)
        self.act = Eng("act", nc.scalar, self.newsem("s_act"), True)
        self.dve = Eng("dve", nc.vector, self.newsem("s_dve"), True)
        self.pool = Eng("pool", nc.gpsimd, self.newsem("s_pool"), True)
        self.sp = Eng("sp", nc.sync, self.newsem("s_sp"), True)
        self.dsems = {}
        self.nwaits = 0
        self.nops = 0

    def newsem(self, name):
        return Sem(self.es.enter_context(self.nc.semaphore(name)), name)

    def dsem(self, *key):
        if key not in self.dsems:
            self.dsems[key] = self.newsem("d_" + "_".join(str(k) for k in key))
        return self.dsems[key]

    def sb(self, name, shape, dt):
        return self.es.enter_context(self.nc.sbuf_tensor(name, list(shape), dt))

    def ps(self, name, shape, dt):
        return self.es.enter_context(self.nc.psum_tensor(name, list(shape), dt))

    def _collect(self, rb, wb):
        need = {}

        def add(s, v):
            if need.get(s, 0) < v:
                need[s] = v

        for b in rb:
            for k in _bks(b):
                for r in self.recs.get((b[0], k), ()):
                    if r[5] == "w" and r[1] < b[2] and b[1] < r[2] and r[3] < b[4] and b[3] < r[4]:
                        add(r[6], r[7])
        for b in wb:
            for k in _bks(b):
                for r in self.recs.get((b[0], k), ()):
                    if r[1] < b[2] and b[1] < r[2] and r[3] < b[4] and b[3] < r[4]:
                        add(r[6], r[7])
        return need

    def _commit(self, rb, wb, sem, val):
        for b in wb:
            rec = [b[0], b[1], b[2], b[3], b[4], "w", sem, val]
            for k in _bks(b):
                lst = self.recs.setdefault((b[0], k), [])
                lst[:] = [r for r in lst if not (b[1] <= r[1] and r[2] <= b[2] and b[3] <= r[3] and r[4] <= b[4])]
                lst.append(rec)
        for b in rb:
            rec = None
            for k in _bks(b):
                lst = self.recs.setdefault((b[0], k), [])
                for r in lst:
                    if r[5] == "r" and r[6] is sem and r[1] == b[1] and r[2] == b[2] and r[3] == b[3] and r[4] == b[4]:
                        r[7] = max(r[7], val)
                        break
                else:
                    if rec is None:
                        rec = [b[0], b[1], b[2], b[3], b[4], "r", sem, val]
                    lst.append(rec)

    def _boxes(self, aps):
        out = []
        for a in aps:
            b = _box(a)
            if b[0] in self.notrack:
                continue
            out.append(b)
        return out

    def _waits(self, eng, need):
        for s, v in need.items():
            if v <= 0:
                continue
            if s is eng.sem and not eng.selfsync:
                continue
            if eng.known.get(s, 0) >= v:
                continue
            eng.h.wait_ge(s.h, v)
            eng.known[s] = v
            self.nwaits += 1

    def op(self, eng, fn, reads=(), writes=(), inc=True):
        rb = self._boxes(reads)
        wb = self._boxes(writes)
        if eng is not self.pe:
            wb = wb + [b for b in rb if b[0].startswith("ps")]
            rb = [b for b in rb if not b[0].startswith("ps")]
        need = self._collect(rb, wb)
        self._waits(eng, need)
        ins = fn()
        self.nops += 1
        if inc:
            eng.sem.cnt += 1
            ins.then_inc(eng.sem.h, 1)
            val = eng.sem.cnt
        else:
            val = eng.sem.cnt + 1
        self._commit(rb, wb, eng.sem, val)
        return ins

    def dma(self, q, out, in_, dsem):
        rb = self._boxes([in_])
        wb = self._boxes([out])
        need = self._collect(rb, wb)
        if dsem.cnt > 0:
            need[dsem] = max(need.get(dsem, 0), dsem.cnt)
        self._waits(q, need)
        ins = q.h.dma_start(out=out, in_=in_)
        ins.then_inc(dsem.h, 16)
        dsem.cnt += 16
        self.nops += 1
        self._commit(rb, wb, dsem, dsem.cnt)

    def mm(self, out, lhsT, rhs, start, stop, inc=None):
        if inc is None:
            inc = stop
        return self.op(self.pe, lambda: self.nc.tensor.matmul(out, lhsT=lhsT, rhs=rhs, start=start, stop=stop),
                       reads=[lhsT, rhs], writes=[out], inc=inc)

    def activation(self, out, in_, func, bias=None, scale=None, extra_reads=()):
        kw = {}
        rd = [in_] + list(extra_reads)
        if bias is not None:
            kw["bias"] = bias
            if not isinstance(bias, (int, float)):
                rd.append(bias)
        if scale is not None:
            kw["scale"] = scale
            if not isinstance(scale, (int, float)):
                rd.append(scale)
        return self.op(self.act, lambda: self.nc.scalar.activation(out=out, in_=in_, func=func, **kw),
                       reads=rd, writes=[out])

    def tt(self, out, in0, in1, op, eng=None):
        eng = eng or self.dve
        return self.op(eng, lambda: eng.h.tensor_tensor(out=out, in0=in0, in1=in1, op=op),
                       reads=[in0, in1], writes=[out])

    def ts(self, out, in0, s1, s2, op0, op1=None, eng=None):
        eng = eng or self.dve
        rd = [in0]
        if not isinstance(s1, (int, float)):
            rd.append(s1)
        if s2 is not None and not isinstance(s2, (int, float)):
            rd.append(s2)
        if op1 is None:
            return self.op(eng, lambda: eng.h.tensor_scalar(out=out, in0=in0, scalar1=s1, scalar2=None, op0=op0),
                           reads=rd, writes=[out])
        return self.op(eng, lambda: eng.h.tensor_scalar(out=out, in0=in0, scalar1=s1, scalar2=s2, op0=op0, op1=op1),
                       reads=rd, writes=[out])

    def stt(self, out, in0, scalar, in1, op0, op1, eng=None):
        eng = eng or self.dve
        rd = [in0, in1]
        if not isinstance(scalar, (int, float)):
            rd.append(scalar)
        return self.op(eng, lambda: eng.h.scalar_tensor_tensor(out=out, in0=in0, scalar=scalar, in1=in1, op0=op0, op1=op1),
                       reads=rd, writes=[out])

    def copy(self, out, in_, eng=None):
        eng = eng or self.dve
        return self.op(eng, lambda: eng.h.tensor_copy(out=out, in_=in_), reads=[in_], writes=[out])

    def acopy(self, out, in_):
        return self.op(self.act, lambda: self.nc.scalar.copy(out=out, in_=in_), reads=[in_], writes=[out])

    def memset(self, ap, val, eng=None):
        eng = eng or self.dve
        return self.op(eng, lambda: eng.h.memset(ap, val), reads=[], writes=[ap])


class Cols:
    def __init__(self):
        self.cols = []
        self.idx = {}

    def add(self, key, vec):
        v = np.zeros(128, np.float32)
        vec = np.asarray(vec, np.float32).reshape(-1)
        v[: vec.shape[0]] = vec
        self.idx[key] = len(self.cols)
        self.cols.append(v)

    def add_chunks(self, key, vec):
        vec = np.asarray(vec, np.float32).reshape(-1, 128)
        for i in range(vec.shape[0]):
            self.add((key, i), vec[i])

    def array(self):
        return np.ascontiguousarray(np.stack(self.cols, axis=1))


def _swap64(v):
    v = np.asarray(v)
    return np.concatenate([v[32:64], v[0:32]])


def make_cols(inp):
    c = Cols()
    for l in range(NL):
        c.add_chunks(("ffn1_norm", l), inp["ffn1_norm"][l])
        c.add_chunks(("mix_norm", l), inp["mix_norm"][l])
        c.add_chunks(("ffn2_norm", l), inp["ffn2_norm"][l])
        for nm in ("a_q_norm", "a_k_norm", "b_q_norm", "b_k_norm", "c_out_norm"):
            g = inp[nm][l]
            c.add((nm, l), np.concatenate([g, g]))
            c.add((nm + "_sw", l), np.concatenate([_swap64(g), _swap64(g)]))
        cw = inp["c_conv"][l]
        for ch in range(9):
            for j in range(5):
                c.add(("conv", l, ch, j), cw[j, ch * 128:(ch + 1) * 128])
        c.add(("alog", l), inp["c_A_log"][l].reshape(-1))
        c.add(("dtb", l), inp["c_dt_bias"][l].reshape(-1))
    c.add("mF", [1.0] * 6 + [0.0] * 6)
    c.add("mB", [0.0] * 6 + [1.0] * 6)
    c.add("one", np.ones(128))
    c.add("eps", np.full(128, EPS))
    return c


NWC = 28 * 128 + 24


def win_perm():
    aq, ak, av, bq, bk, bv, cq, ck, cv, cz, cb, ca = 0, 256, 384, 512, 896, 1280, 1664, 2048, 2432, 2816, 3200, 3212

    def head(base, h):
        return list(range(base + h * 64, base + h * 64 + 64))

    def swp(cols):
        return cols[32:64] + cols[0:32]

    p = []
    qa0 = head(aq, 0) + head(aq, 2)
    qa1 = head(aq, 1) + head(aq, 3)
    p += qa0 + qa1
    p += swp(head(aq, 0)) + swp(head(aq, 2)) + swp(head(aq, 1)) + swp(head(aq, 3))
    ka = head(ak, 0) + head(ak, 1)
    p += ka + swp(head(ak, 0)) + swp(head(ak, 1))
    p += list(range(bq, bq + 384)) + list(range(bk, bk + 384))
    p += list(range(cq, cq + 384)) + list(range(ck, ck + 384)) + list(range(cv, cv + 384)) + list(range(cz, cz + 384))
    p += list(range(av, av + 128)) + list(range(bv, bv + 384))
    p += list(range(cb, cb + 12)) + list(range(ca, ca + 12))
    assert len(p) == NWC
    return np.array(p)


LW = 3072
R0 = 1535
TSW = 2944


def t5_bucket_np(rel):
    rel = np.asarray(rel, np.int64)
    half, exact = 16, 8
    sign = np.where(rel > 0, half, 0)
    n = np.abs(rel)
    nf = np.maximum(n, 1).astype(np.float32)
    large = exact + (np.log(nf / np.float32(exact)) / np.float32(math.log(1024 / exact)) * np.float32(half - exact)).astype(np.int32)
    large = np.minimum(large, half - 1)
    return sign + np.where(n < exact, n, large)


def make_consts():
    c = {}
    I = np.eye(128, dtype=np.float32)
    ii = np.arange(128)
    same64 = (ii[:, None] // 64) == (ii[None, :] // 64)
    same32 = (ii[:, None] // 32) == (ii[None, :] // 32)
    row = ii[:, None]
    col = ii[None, :]
    NEG = -30000.0
    blocks = [
        I,
        same64.astype(np.float32),
        I[::-1].copy(),
        same32.astype(np.float32),
        np.where(same64 & (col < row), 0.0, NEG),
        np.where(same64 & (col > row), 0.0, NEG),
        np.where(same64 & (row <= col), 0.0, NEG),
        np.where(same64 & (row >= col), 0.0, NEG),
        (~same32).astype(np.float32),
    ]
    c["cm"] = np.ascontiguousarray(np.concatenate([b.astype(np.float32) for b in blocks], axis=1))
    sel = np.zeros((12, 12, 128), np.float32)
    for r in range(12):
        sel[r, r, :] = 1.0
    c["sel"] = sel.reshape(12, 12 * 128)
    t = np.arange(S)
    rw = (t // 64).astype(np.float32)
    cl = (t % 64).astype(np.float32)
    inv = (10000.0 ** (-np.arange(16, dtype=np.float32) / 16)).astype(np.float32)
    ang = np.concatenate([rw[:, None] * inv, cl[:, None] * inv], axis=-1).astype(np.float32)
    cos = np.cos(ang).astype(np.float32).T
    sin = np.sin(ang).astype(np.float32).T
    COS = np.concatenate([cos, cos, cos, cos], axis=0)
    SIN = np.concatenate([-sin, sin, -sin, sin], axis=0)
    c["rope"] = np.ascontiguousarray(np.stack([COS, SIN], axis=0).astype(np.float32))
    i = np.arange(LW)
    r = R0 - i
    bkt = t5_bucket_np(r)
    oh = np.zeros((32, LW), np.float32)
    oh[bkt, i] = 1.0
    mult = ((np.abs(r) <= 64).astype(np.float32) + ((r % 4 == 0) & (np.abs(r) <= 256)).astype(np.float32)
            + ((r % 16 == 0) & (np.abs(r) <= 1024)).astype(np.float32))
    c["ohrev"] = oh
    c["multrev"] = np.ascontiguousarray(np.tile(mult[None, :], (6, 1)).astype(np.float32))
    return c


ARN = 50048


class Prog:
    def __init__(self, ncols, colidx, stop_after=None, parts="ABC"):
        self.colidx = colidx
        self.stop_after = stop_after
        self.parts = parts
        self.dbg_sems = []
        self.debug = False
        self.cstop = None
        nc = bass.Bass("TRN2", target_bir_lowering=False)
        self.nc = nc
        self.es = contextlib.ExitStack()
        d = {}

        def inp(name, shape, dt=F32):
            d[name] = nc.dram_tensor(name, list(shape), dt, kind="ExternalInput").ap()

        inp("xT", [D, S])
        inp("cols", [128, ncols])
        for nm in ("ffn1", "ffn2"):
            inp(nm + "_wg", [NL, D, DFF])
            inp(nm + "_wu", [NL, D, DFF])
            inp(nm + "_wd", [NL, DFF, D])
        inp("win", [NL, D, NWC])
        inp("wout", [NL, D, D])
        inp("relb", [32, 6])
        inp("cm", [128, 9 * 128])
        inp("sel", [12, 12 * 128])
        inp("rope", [2, 128, S])
        inp("ohrev", [32, LW])
        inp("multrev", [6, LW])
        self.inputs = dict(d)
        d["yT"] = nc.dram_tensor("yT", [D, S], F32, kind="ExternalOutput").ap()
        d["wrev"] = nc.dram_tensor("wrev", [6, LW], BF16, kind="Internal").ap()
        d["strips"] = nc.dram_tensor("strips", [6, 128, TSW], BF16, kind="Internal").ap()
        self.d = d
        self.ncols = ncols

    def col(self, key, n=128):
        i = self.colidx[key]
        return self.COLS[0:n, i:i + 1]

    def build(self):
        nc = self.nc
        with self.es:
            kb = KB(nc, self.es)
            self.kb = kb
            for k in self.inputs:
                kb.notrack.add(self.inputs[k].tensor.name)
            self.alloc()
            self.body()
        return nc

    def areset(self, base=0):
        self.aptr = base

    def aalloc(self, shape, dt):
        n = 1
        for x in shape[1:]:
            n *= x
        ne = n * (2 if dt == F32 else 1)
        off = self.aptr
        off += off % 2
        assert off + ne <= ARN, ("arena overflow", off, ne)
        self.aptr = off + ne
        v = self.AR[:, off:off + ne]
        if dt == F32:
            v = v.bitcast(F32)
        v = v[0:shape[0]]
        if len(shape) == 3:
            v = v.rearrange("p (a b) -> p a b", a=shape[1])
        return v

    def alloc(self):
        kb = self.kb
        self.H = kb.sb("H", [128, 8, S], F32)
        self.HN = kb.sb("HN", [128, 8, S], BF16)
        self.COLS = kb.sb("COLS", [128, self.ncols], F32)
        self.CMF = kb.sb("CMF", [128, 9, 128], F32)
        self.CMB = kb.sb("CMB", [128, 9, 128], BF16)
        self.SELF = kb.sb("SELF", [12, 12, 128], F32)
        self.ONES = kb.sb("ONES", [128, 128], BF16)
        self.AR = kb.sb("AR", [128, ARN], BF16)
        self.PS = [kb.ps("ps%d" % i, [128, 512], F32) for i in range(8)]

    def ffn_alloc(self):
        self.areset()
        self.WG = [self.aalloc([128, 8, 256], BF16) for i in range(2)]
        self.WU = [self.aalloc([128, 8, 256], BF16) for i in range(2)]
        self.WD = [self.aalloc([128, D], BF16) for i in range(8)]
        self.HID = self.aalloc([128, 8, S], BF16)
        self.SG = [self.aalloc([128, 512], F32) for i in range(4)]
        self.SQ = self.aalloc([128, 8, 512], BF16)
        self.RSTD = self.aalloc([128, 512], F32)

    def done(self, tag):
        return self.stop_after == tag

    def dump(self, name, ap):
        if not getattr(self, "debug", False):
            return
        t = self.nc.dram_tensor("dbg_" + name, list(ap.shape), F32, kind="ExternalOutput").ap()
        sm = self.kb.dsem("dbg", name)
        self.kb.dma(self.kb.pool, t, ap, sm)
        self.dbg_sems.append(sm)

    def body(self):
        kb = self.kb
        nc = self.nc
        d = self.d
        kb.dma(kb.sp, self.COLS[:, :], d["cols"], kb.dsem("cols"))
        kb.dma(kb.sp, self.CMF[:, :, :], d["cm"].rearrange("p (a b) -> p a b", a=9), kb.dsem("cm"))
        kb.dma(kb.sp, self.SELF[:, :, :], d["sel"].rearrange("p (a b) -> p a b", a=12), kb.dsem("sel"))
        xv = d["xT"].rearrange("(dc p) t -> p dc t", p=128)
        for dc in range(8):
            kb.dma(kb.sp, self.H[:, dc, :], xv[:, dc, :], kb.dsem("x", dc % 4))
        kb.memset(self.ONES[:, :], 1.0)
        kb.copy(self.CMB[:, :, :], self.CMF[:, :, :])
        self.IDB = self.CMB[:, 0, :]
        self.BD64B = self.CMB[:, 1, :]
        if "B" in self.parts:
            self.build_strips()
        for l in range(NL):
            self.ffn_alloc()
            if not getattr(self, "skip_ffn", False):
                self.rmsnorm(("ffn1_norm", l))
                self.ffn(d["ffn1_wg"][l], d["ffn1_wu"][l], d["ffn1_wd"][l])
            if self.done(("ffn1", l)):
                break
            self.rmsnorm(("mix_norm", l))
            self.mixer(l)
            if self.done(("mix", l)):
                break
            self.ffn_alloc()
            self.rmsnorm(("ffn2_norm", l))
            self.ffn(d["ffn2_wg"][l], d["ffn2_wu"][l], d["ffn2_wd"][l])
            if self.done(("ffn2", l)):
                break
        yv = d["yT"].rearrange("(dc p) t -> p dc t", p=128)
        osems = []
        for dc in range(8):
            sm = kb.dsem("y", dc)
            kb.dma(kb.sp, yv[:, dc, :], self.H[:, dc, :], sm)
            osems.append(sm)
        for sm in osems + self.dbg_sems:
            nc.sync.wait_ge(sm.h, sm.cnt)

    def rsqrt(self, out, in_, scale):
        kb = self.kb
        n = out.shape[0]
        kb.activation(out, in_, AF.Sqrt, bias=self.col("eps", n), scale=scale)
        kb.op(kb.dve, lambda: self.nc.vector.reciprocal(out=out, in_=out), reads=[out], writes=[out])

    def rmsnorm(self, gkey):
        kb = self.kb
        for tc in range(4):
            tsl = slice(tc * 512, (tc + 1) * 512)
            for dc in range(8):
                kb.activation(self.SQ[:, dc, :], self.H[:, dc, tsl], AF.Square)
            bank = self.PS[tc % 2]
            for dc in range(8):
                kb.mm(bank[:, :], self.ONES[:, :], self.SQ[:, dc, :], start=(dc == 0), stop=(dc == 7))
            self.rsqrt(self.RSTD[:, :], bank[:, :], 1.0 / D)
            for dc in range(8):
                kb.stt(self.HN[:, dc, tsl], self.H[:, dc, tsl], self.col((gkey, dc)), self.RSTD[:, :],
                       ALU.mult, ALU.mult)

    def ffn(self, wg, wu, wd):
        kb = self.kb
        wgv = wg.rearrange("(kc p) f -> p kc f", p=128)
        wuv = wu.rearrange("(kc p) f -> p kc f", p=128)
        wdv = wd.rearrange("(f p) d -> p f d", p=128)

        def load_pair(p):
            s = p % 2
            kb.dma(kb.pool, self.WG[s][:, :, :], wgv[:, :, p * 256:(p + 1) * 256], kb.dsem("wg", s))
            kb.dma(kb.pool, self.WU[s][:, :, :], wuv[:, :, p * 256:(p + 1) * 256], kb.dsem("wu", s))

        load_pair(0)
        load_pair(1)
        for (f0, f1) in GROUPS:
            for i, f in enumerate(range(f0, f1)):
                kb.dma(kb.pool, self.WD[i][:, :], wdv[:, f, :], kb.dsem("wd", i))
            for f in range(f0, f1):
                p = f // 2
                s = p % 2
                csl = slice((f % 2) * 128, (f % 2) * 128 + 128)
                for (W, b0) in ((self.WG[s], 0), (self.WU[s], 4)):
                    for kc in range(8):
                        for tc in range(4):
                            kb.mm(self.PS[b0 + tc][:, :], W[:, kc, csl], self.HN[:, kc, tc * 512:(tc + 1) * 512],
                                  start=(kc == 0), stop=(kc == 7))
                if f % 2 == 1 and p + 2 < NF // 2:
                    load_pair(p + 2)
                for tc in range(4):
                    kb.activation(self.SG[tc][:, :], self.PS[tc][:, :], AF.Silu)
                    kb.tt(self.HID[:, f - f0, tc * 512:(tc + 1) * 512], self.SG[tc][:, :], self.PS[4 + tc][:, :],
                          ALU.mult)
            nfl = f1 - f0
            for dc in range(8):
                b0 = (dc % 2) * 4
                for fl in range(nfl):
                    for tc in range(4):
                        kb.mm(self.PS[b0 + tc][:, :], self.WD[fl][:, dc * 128:(dc + 1) * 128],
                              self.HID[:, fl, tc * 512:(tc + 1) * 512], start=(fl == 0), stop=(fl == nfl - 1))
                for tc in range(4):
                    hs = self.H[:, dc, tc * 512:(tc + 1) * 512]
                    kb.stt(hs, self.PS[b0 + tc][:, :], 0.5, hs, ALU.mult, ALU.add)

    def load_win(self, l, dst, c0, ncol, key):
        src = self.d["win"][l].rearrange("(kc p) c -> p kc c", p=128)[:, :, c0:c0 + ncol]
        self.kb.dma(self.kb.pool, dst, src, self.kb.dsem("win", key))

    def proj(self, bank, W, c0, tc, ncol=128):
        kb = self.kb
        for kc in range(8):
            kb.mm(bank[0:ncol, :], W[:, kc, c0:c0 + ncol], self.HN[:, kc, tc * 512:(tc + 1) * 512],
                  start=(kc == 0), stop=(kc == 7))

    def wout(self, l, mcs, srcs):
        kb = self.kb
        wo = []
        for i, mc in enumerate(mcs):
            w = self.aalloc([128, D], BF16)
            kb.dma(kb.pool, w[:, :], self.d["wout"][l][mc * 128:(mc + 1) * 128, :], kb.dsem("wo", i))
            wo.append(w)
        n = len(mcs)
        for dc in range(8):
            b0 = (dc % 2) * 4
            for i in range(n):
                for tc in range(4):
                    kb.mm(self.PS[b0 + tc][:, :], wo[i][:, dc * 128:(dc + 1) * 128],
                          srcs[i][:, tc * 512:(tc + 1) * 512], start=(i == 0), stop=(i == n - 1))
            for tc in range(4):
                hs = self.H[:, dc, tc * 512:(tc + 1) * 512]
                kb.tt(hs, self.PS[b0 + tc][:, :], hs, ALU.add)

    def seg2(self, base_ap, delta):
        a = base_ap.ap
        return bass.AP(base_ap.tensor, base_ap.offset, [[a[0][0], a[0][1]], [delta, 2], [1, 64]])

    def mixer(self, l):
        self.areset()
        if "A" in self.parts or "B" in self.parts:
            self.vtok(l)
        base = self.aptr
        if "A" in self.parts:
            self.areset(base)
            self.mixer_a(l)
        if "B" in self.parts:
            self.areset(base)
            self.mixer_b(l)
        if "C" in self.parts:
            self.areset()
            self.mixer_c(l)

    def vtok(self, l):
        kb = self.kb
        self.VT = self.aalloc([128, 16, 1024], BF16)
        mark = self.aptr
        WV = self.aalloc([128, 8, 512], BF16)
        self.load_win(l, WV[:, :, :], 24 * 128, 512, "wv")
        kb.memset(self.VT.rearrange("p m (s c) -> p (m s) c", s=8)[:, :, 64:128], 1.0)
        for m in range(16):
            bank = self.PS[m % 2]
            for kc in range(8):
                kb.mm(bank[:, :], self.HN[:, kc, m * 128:(m + 1) * 128], WV[:, kc, :], start=(kc == 0), stop=(kc == 7))
            vdst = self.VT[:, m, :].rearrange("p (s c) -> p s c", s=8)[:, :, 0:64]
            vsrc = bank[:, :].rearrange("p (s c) -> p s c", s=8)
            if m % 2 == 0:
                kb.copy(vdst, vsrc)
            else:
                kb.acopy(vdst, vsrc)
        self.aptr = mark + 0

    def attention(self, qsrc, ksrc, vcol, dst, strip=None):
        kb = self.kb
        for qc in range(4):
            qsl = slice(qc * 512, (qc + 1) * 512)
            psO = self.PS[4 + qc % 2]
            kts = []
            for kt in range(16):
                dk = kt * 128 - qc * 512
                if strip is not None and not (-1024 <= dk <= 1408):
                    continue
                kts.append(kt)
            NB = 4
            LA = 3

            def emit_qk(n):
                kt = kts[n]
                psS = self.PS[n % NB]
                pt = self.PT[n % NB]
                kb.mm(psS[:, :], ksrc[:, kt * 128:(kt + 1) * 128], qsrc[:, qsl], start=True, stop=True)
                kb.activation(pt[:, :], psS[:, :], AF.Exp, scale=0.125)
                if strip is not None:
                    cs = 1408 - (kt * 128 - qc * 512)
                    kb.tt(pt[:, :], pt[:, :], strip[:, cs:cs + 512], ALU.mult)

            def emit_pv(n):
                kt = kts[n]
                vl = self.VT[:, kt, vcol * 2:vcol * 2 + 128]
                kb.mm(psO[:, :], vl, self.PT[n % NB][:, :], start=(n == 0), stop=(n == len(kts) - 1))

            for n in range(len(kts) + LA):
                if n < len(kts):
                    emit_qk(n)
                if n >= LA:
                    emit_pv(n - LA)
            kb.op(kb.dve, lambda: self.nc.vector.reciprocal(out=self.REC[0:64, :], in_=psO[64:128, :]),
                  reads=[psO[64:128, :]], writes=[self.REC[0:64, :]])
            kb.tt(dst[:, qsl], psO[0:64, :], self.REC[0:64, :], ALU.mult)

    def qknorm_chunk(self, l, W, c0, cs0, gkey, dst, tc, rope):
        kb = self.kb
        tsl = slice(tc * 512, (tc + 1) * 512)
        px = self.PS[0]
        self.proj(px, W, c0, tc)
        kb.activation(self.SQT[:, :], px[:, :], AF.Square)
        pst = self.PS[2]
        kb.mm(pst[:, :], self.BD64B, self.SQT[:, :], start=True, stop=True)
        self.rsqrt(self.RST[:, :], pst[:, :], 1.0 / 64)
        if not rope:
            kb.stt(dst[:, tsl], px[:, :], self.col((gkey, l)), self.RST[:, :], ALU.mult, ALU.mult)
            return
        pw = self.PS[1]
        self.proj(pw, W, cs0, tc)
        kb.stt(self.T0[:, :], px[:, :], self.col((gkey, l)), self.RST[:, :], ALU.mult, ALU.mult)
        kb.stt(self.T1[:, :], pw[:, :], self.col((gkey + "_sw", l)), self.RST[:, :], ALU.mult, ALU.mult)
        kb.tt(self.T0[:, :], self.T0[:, :], self.CS[:, 0, :], ALU.mult)
        kb.tt(self.T1[:, :], self.T1[:, :], self.CS[:, 1, :], ALU.mult)
        kb.tt(dst[:, tsl], self.T0[:, :], self.T1[:, :], ALU.add)

    def att_tmps(self):
        self.PT = [self.aalloc([128, 512], BF16) for i in range(4)]
        self.REC = self.aalloc([128, 512], F32)
        self.SQT = self.aalloc([128, 512], BF16)
        self.RST = self.aalloc([128, 512], F32)
        self.T0 = self.aalloc([128, 512], F32)
        self.T1 = self.aalloc([128, 512], F32)

    def mixer_a(self, l):
        kb = self.kb
        self.att_tmps()
        self.CS = self.aalloc([128, 2, 512], F32)
        WA = self.aalloc([128, 8, 768], BF16)
        self.load_win(l, WA[:, :, :], 0, 768, "wa")
        QT = [self.aalloc([128, S], BF16) for i in range(2)]
        KT = self.aalloc([128, S], BF16)
        OUT = [self.aalloc([128, S], BF16) for i in range(2)]
        rv = self.d["rope"].rearrange("c p t -> p c t")
        for tc in range(4):
            kb.dma(kb.sp, self.CS[:, :, :], rv[:, :, tc * 512:(tc + 1) * 512], kb.dsem("rope"))
            self.qknorm_chunk(l, WA, 0, 256, "a_q_norm", QT[0], tc, True)
            self.qknorm_chunk(l, WA, 128, 384, "a_q_norm", QT[1], tc, True)
            self.qknorm_chunk(l, WA, 512, 640, "a_k_norm", KT, tc, True)
        for h in range(4):
            g = h // 2
            self.attention(QT[h % 2][g * 64:(g + 1) * 64, :], KT[g * 64:(g + 1) * 64, :], g * 64,
                           OUT[h // 2][(h % 2) * 64:(h % 2) * 64 + 64, :])
        self.wout(l, [0, 1], OUT)

    def build_strips(self):
        kb = self.kb
        self.areset()
        d = self.d
        OH = self.aalloc([32, LW], F32)
        MU = self.aalloc([6, LW], F32)
        RB = self.aalloc([32, 6], F32)
        W6 = self.aalloc([6, LW], F32)
        W6B = self.aalloc([6, LW], BF16)
        kb.dma(kb.sp, OH[:, :], d["ohrev"], kb.dsem("oh"))
        kb.dma(kb.sp, MU[:, :], d["multrev"], kb.dsem("mu"))
        kb.dma(kb.sp, RB[:, :], d["relb"], kb.dsem("rb"))
        for n in range(LW // 512):
            sl = slice(n * 512, (n + 1) * 512)
            bank = self.PS[n % 2]
            kb.mm(bank[0:6, :], RB[:, :], OH[:, sl], start=True, stop=True)
            kb.activation(W6[:, sl], bank[0:6, :], AF.Exp)
        kb.tt(W6B[:, :], W6[:, :], MU[:, :], ALU.mult)
        kb.dma(kb.sp, d["wrev"], W6B[:, :], kb.dsem("wrev"))
        HK = self.aalloc([128, TSW], BF16)
        TSB = self.aalloc([128, TSW], BF16)
        JB = self.CMB[:, 2, :]
        for h in range(6):
            src = bass.AP(d["wrev"].tensor, h * LW, [[1, 128], [1, TSW]])
            kb.dma(kb.sp, HK[:, :], src, kb.dsem("hk"))
            for n in range((TSW + 511) // 512):
                w = min(512, TSW - n * 512)
                sl = slice(n * 512, n * 512 + w)
                bank = self.PS[n % 2]
                kb.mm(bank[:, 0:w], JB, HK[:, sl], start=True, stop=True)
                if n % 2 == 0:
                    kb.copy(TSB[:, sl], bank[:, 0:w])
                else:
                    kb.acopy(TSB[:, sl], bank[:, 0:w])
            kb.dma(kb.sp, d["strips"][h], TSB[:, :], kb.dsem("strips"))

    def mixer_b(self, l):
        kb = self.kb
        self.att_tmps()
        TS = [self.aalloc([128, TSW], BF16) for i in range(2)]
        WB = [self.aalloc([128, 8, 256], BF16) for i in range(2)]
        QB = [self.aalloc([128, S], BF16) for i in range(2)]
        KBf = [self.aalloc([128, S], BF16) for i in range(2)]
        OUT = [self.aalloc([128, S], BF16) for i in range(2)]
        mark = self.aptr
        for hp in range(3):
            s = hp % 2
            self.load_win(l, WB[s][:, :, 0:128], (6 + hp) * 128, 128, ("wbq", s))
            self.load_win(l, WB[s][:, :, 128:256], (9 + hp) * 128, 128, ("wbk", s))
            for tc in range(4):
                self.qknorm_chunk(l, WB[s], 0, None, "b_q_norm", QB[s], tc, False)
                self.qknorm_chunk(l, WB[s], 128, None, "b_k_norm", KBf[s], tc, False)
            for hh in range(2):
                h = 2 * hp + hh
                kb.dma(kb.sp, TS[hh][:, :], self.d["strips"][h], kb.dsem("ts", hh))
                self.attention(QB[s][hh * 64:(hh + 1) * 64, :], KBf[s][hh * 64:(hh + 1) * 64, :], 128 + h * 64,
                               OUT[s][hh * 64:(hh + 1) * 64, :], strip=TS[hh])
            self.aptr = mark
            self.wout(l, [2 + hp], [OUT[s]])

    def bcast_mid(self, ap2d, n):
        a = ap2d.ap
        return bass.AP(ap2d.tensor, ap2d.offset, [[a[0][0], a[0][1]], [0, n], [a[1][0], a[1][1]]])

    def bcast_last(self, ap, n):
        a = [list(x) for x in ap.ap]
        a[-1] = [0, n]
        return bass.AP(ap.tensor, ap.offset, a)

    def c_gates(self, l):
        kb = self.kb
        nc = self.nc
        GW = self.aalloc([128, 8, 24], BF16)
        self.load_win(l, GW[:, :, :], 28 * 128, 24, "wg12")
        self.GC = self.aalloc([12, S], F32)
        self.EG = self.aalloc([12, S], F32)
        self.TOKT = self.aalloc([128, 5, 16 * 12], F32)
        self.TOT = self.aalloc([12, 32], F32)
        self.DEC = self.aalloc([12, 32], F32)
        NEGA = self.aalloc([12, 2], F32)
        mark = self.aptr
        B0 = self.aalloc([12, S], F32)
        B1 = self.aalloc([12, S], F32)
        B2 = self.aalloc([12, S], F32)
        B3 = self.aalloc([12, S], F32)
        B4 = self.aalloc([12, S], F32)
        kb.activation(NEGA[:, 0:1], self.col(("alog", l), 12), AF.Exp)
        kb.ts(NEGA[:, 1:2], NEGA[:, 0:1], -1.0, 0.0, ALU.mult, ALU.add)
        for tc in range(4):
            tsl = slice(tc * 512, (tc + 1) * 512)
            pb_, pa_ = self.PS[0], self.PS[1]
            self.proj(pb_, GW, 0, tc, ncol=12)
            self.proj(pa_, GW, 12, tc, ncol=12)
            kb.activation(B0[:, tsl], pb_[0:12, :], AF.Sigmoid)
            kb.activation(B4[:, tsl], pa_[0:12, :], AF.Exp, bias=self.col(("dtb", l), 12))
            kb.activation(B4[:, tsl], B4[:, tsl], AF.Ln, bias=self.col("one", 12))
            kb.ts(B1[:, tsl], B4[:, tsl], NEGA[:, 1:2], 0.0, ALU.mult, ALU.add)

        if self.cstop == "g2":
            return

        def v3(b):
            return b.rearrange("p (c t) -> p c t", c=32)

        src = B1
        seq = [B2, B3, B2, B3, B2, B3]
        for k, sh in enumerate((1, 2, 4, 8, 16, 32)):
            dst = seq[k]
            kb.tt(v3(dst)[:, :, sh:64], v3(src)[:, :, sh:64], v3(src)[:, :, 0:64 - sh], ALU.add)
            kb.copy(v3(dst)[:, :, 0:sh], v3(src)[:, :, 0:sh])
            src = dst
        P = B3
        if self.cstop == "g3":
            return
        kb.copy(self.TOT[:, :], v3(P)[:, :, 63])
        totb = self.bcast_last(self.TOT[:, :].rearrange("p (c o) -> p c o", o=1), 64)
        kb.tt(v3(B2), totb, v3(P), ALU.subtract)
        kb.tt(B2[:, :], B2[:, :], B1[:, :], ALU.add)
        kb.ts(B1[:, :], P[:, :], self.col("mF", 12), 0.0, ALU.mult, ALU.add)
        kb.stt(self.GC[:, :], B2[:, :], self.col("mB", 12), B1[:, :], ALU.mult, ALU.add)
        kb.activation(self.DEC[:, :], self.TOT[:, :], AF.Exp)
        kb.tt(v3(B1), totb, v3(self.GC), ALU.subtract)
        kb.activation(B1[:, :], B1[:, :], AF.Exp)
        kb.activation(B2[:, :], B0[:, :], AF.Ln)
        kb.tt(B2[:, :], B2[:, :], self.GC[:, :], ALU.add)
        kb.ts(B3[:, :], self.GC[:, :], -1.0, 0.0, ALU.mult, ALU.add)
        kb.activation(self.EG[:, :], self.GC[:, :], AF.Exp)
        kb.tt(B4[:, :], self.EG[:, :], B0[:, :], ALU.mult)
        if self.cstop == "g4":
            return
        IDF = self.CMF[0:12, 0, 0:12]
        for q, X in enumerate((B2, B3, B4, B0, B1)):
            bank = self.PS[2 + q % 2]
            for m in range(16):
                kb.mm(bank[:, m * 12:(m + 1) * 12], X[0:12, m * 128:(m + 1) * 128], IDF, start=True, stop=True)
            kb.copy(self.TOKT[:, q, :], bank[:, 0:192])
        self.aptr = mark

    def tok(self, q, tile, r):
        return self.TOKT[:, q, tile * 12 + r:tile * 12 + r + 1]

    def tokb(self, q, r):
        base = self.TOKT[:, q, r:r + 1]
        a = base.ap
        return bass.AP(base.tensor, base.offset, [[a[0][0], a[0][1]], [12, 16], [0, 64]])

    def mixer_c(self, l):
        kb = self.kb
        self.c_gates(l)
        if self.cstop in ("g2", "g3", "g4"):
            return
        self.dump("GC", self.GC[:, :])
        self.dump("TOKT", self.TOKT[:, :, :])
        self.dump("DEC", self.DEC[:, :])
        if self.cstop == "gates":
            return
        base_pair = self.aptr
        for hp in range(3):
            self.areset(base_pair)
            self.c_pair(l, hp)

    def c_pair(self, l, hp):
        kb = self.kb
        nc = self.nc
        QT = self.aalloc([128, S], BF16)
        KT = self.aalloc([128, S], BF16)
        KTOK = self.aalloc([128, 16, 128], BF16)
        VTOK = self.aalloc([128, 16, 128], BF16)
        ON = self.aalloc([128, 16, 128], BF16)
        self.SQT = self.aalloc([128, 512], BF16)
        self.RST = self.aalloc([128, 512], F32)
        mark = self.aptr
        WC = self.aalloc([128, 8, 384], BF16)
        XS = self.aalloc([128, S + 4], F32)
        ACC = self.aalloc([128, S], F32)
        VTf = self.aalloc([128, S], BF16)
        for ci in range(3):
            self.load_win(l, WC[:, :, ci * 128:(ci + 1) * 128], (12 + 3 * ci + hp) * 128, 128, ("wc", ci))
        kb.memset(XS[:, 0:2], 0.0)
        kb.memset(XS[:, S + 2:S + 4], 0.0)
        for ci, kind in enumerate("qkv"):
            for tc in range(4):
                bank = self.PS[tc % 2]
                self.proj(bank, WC, ci * 128, tc)
                if tc % 2 == 0:
                    kb.copy(XS[:, 2 + tc * 512:2 + (tc + 1) * 512], bank[:, :])
                else:
                    kb.acopy(XS[:, 2 + tc * 512:2 + (tc + 1) * 512], bank[:, :])
            ch = ci * 3 + hp
            kb.ts(ACC[:, :], XS[:, 0:S], self.col(("conv", l, ch, 0)), 0.0, ALU.mult, ALU.add)
            for j in range(1, 5):
                kb.stt(ACC[:, :], XS[:, j:j + S], self.col(("conv", l, ch, j)), ACC[:, :], ALU.mult, ALU.add)
            if kind == "v":
                kb.activation(VTf[:, :], ACC[:, :], AF.Silu)
                continue
            kb.activation(ACC[:, :], ACC[:, :], AF.Silu)
            dst = QT if kind == "q" else KT
            for tc in range(4):
                tsl = slice(tc * 512, (tc + 1) * 512)
                kb.activation(self.SQT[:, :], ACC[:, tsl], AF.Square)
                pst = self.PS[2 + tc % 2]
                kb.mm(pst[:, :], self.BD64B, self.SQT[:, :], start=True, stop=True)
                self.rsqrt(self.RST[:, :], pst[:, :], 1.0)
                kb.stt(dst[:, tsl], ACC[:, tsl], 0.125 if kind == "q" else 1.0, self.RST[:, :], ALU.mult, ALU.mult)
        for (src, dstk) in ((KT, KTOK), (VTf, VTOK)):
            for g4 in range(4):
                bank = self.PS[4 + g4 % 2]
                for m4 in range(4):
                    m = g4 * 4 + m4
                    kb.mm(bank[:, m4 * 128:(m4 + 1) * 128], src[:, m * 128:(m + 1) * 128], self.IDB, start=True, stop=True)
                if g4 % 2 == 0:
                    kb.copy(dstk[:, g4 * 4:(g4 + 1) * 4, :], bank[:, :].rearrange("p (a b) -> p a b", a=4))
                else:
                    kb.acopy(dstk[:, g4 * 4:(g4 + 1) * 4, :], bank[:, :].rearrange("p (a b) -> p a b", a=4))
        if hp == 0:
            self.dump("QT", QT[:, :])
            self.dump("KT", KT[:, :])
            self.dump("KTOK", KTOK[:, :, :])
            self.dump("VTOK", VTOK[:, :, :])
        if self.cstop == "conv":
            return
        for hh in range(2):
            self.areset(mark)
            if self.cstop in ("t1", "t1a", "t1b", "t2", "t3", "t4", "t5") and (hp, hh) != (0, 0):
                continue
            self.c_head(l, hp, hh, QT, KT, KTOK, VTOK, ON)
        if self.cstop is not None:
            return
        self.areset(mark)
        WZ = self.aalloc([128, 8, 128], BF16)
        OUTC = self.aalloc([128, S], BF16)
        SZ = self.aalloc([128, 512], F32)
        self.load_win(l, WZ[:, :, :], (21 + hp) * 128, 128, "wz")
        for tc in range(4):
            tsl = slice(tc * 512, (tc + 1) * 512)
            pz = self.PS[tc % 2]
            self.proj(pz, WZ, 0, tc)
            kb.activation(SZ[:, :], pz[:, :], AF.Silu)
            pt = self.PS[2 + tc % 2]
            for m4 in range(4):
                m = tc * 4 + m4
                kb.mm(pt[:, m4 * 128:(m4 + 1) * 128], ON[:, m, :], self.IDB, start=True, stop=True)
            kb.stt(OUTC[:, tsl], pt[:, :], self.col(("c_out_norm", l)), SZ[:, :], ALU.mult, ALU.mult)
        self.wout(l, [5 + hp], [OUTC])

    def c_head(self, l, hp, hh, QT, KT, KTOK, VTOK, ON):
        kb = self.kb
        nc = self.nc
        h = 2 * hp + hh
        bq = hh * 64
        AT = [self.aalloc([128, 16, 128], BF16) for i in range(2)]
        U = [self.aalloc([128, 16, 64], BF16) for i in range(2)]
        KD = [self.aalloc([128, 16, 64], BF16) for i in range(2)]
        WT = self.aalloc([128, S], BF16)
        QGT = self.aalloc([128, S], BF16)
        DECB = self.aalloc([128, 2, 32], F32)
        SF = self.aalloc([128, 64], F32)
        SBb = self.aalloc([128, 64], BF16)
        VN = [self.aalloc([128, 64], BF16) for i in range(4)]
        G = 2
        NG = 16 // G
        mark_t = self.aptr
        OF = [self.aalloc([128, 16, 64], F32) for i in range(2)]
        self.aptr = mark_t
        KBGs = [self.aalloc([128, 16, 64], BF16) for i in range(2)]
        VBs = [self.aalloc([128, 16, 64], BF16) for i in range(2)]
        names = ["DN", "DT", "NN", "NT", "Q0", "Q0P", "CN", "R0", "QA", "QPA", "QB", "QPB", "R1"]
        tbs = []
        XNs, XTs = [], []
        for dr in range(2):
            t = {n: self.aalloc([128, G, 128], BF16) for n in names}
            t["T0"], t["Z"], t["TT"] = t["QA"], t["QPA"], t["QB"]
            tbs.append(t)
            XNs.append(self.aalloc([128, G, 128], F32))
            XTs.append(self.aalloc([128, G, 128], F32))
        tb = tbs[1]

        def f2(t):
            return t.rearrange("p a b -> p (a b)")

        IDB = self.IDB
        GW = G * 128
        for dr in range(2):
            r = dr * 6 + h
            kb.tt(KBGs[dr][:, :, :], KTOK[:, :, bq:bq + 64], self.tokb(2, r), ALU.mult)
            kb.tt(VBs[dr][:, :, :], VTOK[:, :, bq:bq + 64], self.tokb(3, r), ALU.mult)
            kb.tt(KD[dr][:, :, :], KTOK[:, :, bq:bq + 64], self.tokb(4, r), ALU.mult)
            pdc = self.PS[7]
            kb.mm(pdc[:, 0:32], self.SELF[0:12, r, :], self.DEC[0:12, :], start=True, stop=True)
            kb.copy(DECB[:, dr, :], pdc[:, 0:32])

        def titer(dr, gi):
            r = dr * 6 + h
            sb_ = dr * 64
            T = tbs[dr]
            XN, XT = XNs[dr], XTs[dr]
            KBG, VB = KBGs[dr], VBs[dr]
            B = [self.PS[dr * 4 + i] for i in range(4)]
            tsl = slice(gi * GW, (gi + 1) * GW)

            def mmg(bank, lt, rt):
                for m in range(G):
                    kb.mm(bank[:, m * 128:(m + 1) * 128], lt[:, m, :], rt[:, m, :] if rt is not None else IDB,
                          start=True, stop=True)

            psG, psE, psK, psQ = B[0], B[1], B[2], B[3]
            kb.mm(psG[:, 0:GW], self.SELF[0:12, r, :], self.GC[0:12, tsl], start=True, stop=True)
            kb.mm(psE[:, 0:GW], self.SELF[0:12, r, :], self.EG[0:12, tsl], start=True, stop=True)
            for m in range(G):
                tk = slice((gi * G + m) * 128, (gi * G + m + 1) * 128)
                bl = slice(m * 128, (m + 1) * 128)
                kb.mm(psK[:, bl], KT[bq:bq + 64, tk], KT[bq:bq + 64, tk], start=True, stop=True)
            for m in range(G):
                tk = slice((gi * G + m) * 128, (gi * G + m + 1) * 128)
                bl = slice(m * 128, (m + 1) * 128)
                kb.mm(psQ[:, bl], KT[bq:bq + 64, tk], QT[bq:bq + 64, tk], start=True, stop=True)
            yield
            for m in range(G):
                kb.stt(XN[:, m, :], psG[:, m * 128:(m + 1) * 128], -1.0, self.CMF[:, 4 + dr, :], ALU.mult, ALU.add)
                kb.tt(XT[:, m, :], psG[:, m * 128:(m + 1) * 128], self.CMF[:, 6 + dr, :], ALU.add)
            kb.tt(QGT[sb_:sb_ + 64, tsl], QT[bq:bq + 64, tsl], psE[bq:bq + 64, 0:GW], ALU.mult)
            for m in range(G):
                tile = gi * G + m
                kb.activation(T["DN"][:, m, :], XN[:, m, :], AF.Exp, bias=self.tok(0, tile, r))
                kb.activation(T["DT"][:, m, :], XT[:, m, :], AF.Exp, bias=self.tok(1, tile, r))
            yield
            kb.stt(f2(T["NN"]), psK[:, 0:GW], -1.0, f2(T["DN"]), ALU.mult, ALU.mult)
            kb.tt(f2(AT[dr][:, gi * G:(gi + 1) * G, :]), psQ[:, 0:GW], f2(T["DT"]), ALU.mult)
            mmg(B[0], T["NN"], None)
            yield
            kb.acopy(f2(T["NT"]), B[0][:, 0:GW])
            pe = kb.pool
            kb.tt(T["Q0P"][:, :, :], T["NN"][:, :, :], self.bcast_mid(self.CMB[:, 3, :], G), ALU.mult, eng=pe)
            kb.tt(T["CN"][:, :, :], T["NN"][:, :, :], self.bcast_mid(self.CMB[:, 8, :], G), ALU.mult, eng=pe)
            yield
            kb.tt(T["Q0"][:, :, :], T["NT"][:, :, :], self.bcast_mid(self.CMB[:, 3, :], G), ALU.mult, eng=pe)
            kb.tt(T["R0"][:, :, :], T["Q0"][:, :, :], self.bcast_mid(self.CMB[:, 0, :], G), ALU.add, eng=pe)
            yield
            Q, QP, R = T["Q0"], T["Q0P"], T["R0"]
            alt = [(T["QA"], T["QPA"]), (T["QB"], T["QPB"])]
            for k in range(1, 5):
                Qn, QPn = alt[(k - 1) % 2]
                Rn = T["R1"] if k % 2 == 1 else T["R0"]
                pA, pB, pC = B[1], B[2], B[3]
                if k < 4:
                    mmg(pA, QP, Q)
                mmg(pB, Q, QP)
                yield
                if k < 4:
                    kb.acopy(f2(Qn), pA[:, 0:GW])
                if k % 2 == 0:
                    kb.acopy(f2(QPn), pB[:, 0:GW])
                else:
                    kb.copy(f2(QPn), pB[:, 0:GW])
                yield
                mmg(pC, QPn, R)
                yield
                kb.tt(f2(Rn), pC[:, 0:GW], f2(R), ALU.add)
                yield
                Q, QP, R = Qn, QPn, Rn
            pD, pE, pF = B[0], B[1], B[2]
            mmg(pD, R, None)
            mmg(pE, T["CN"], R)
            yield
            kb.acopy(f2(T["T0"]), pD[:, 0:GW])
            kb.copy(f2(T["Z"]), pE[:, 0:GW])
            yield
            mmg(pF, T["T0"], T["Z"])
            yield
            kb.tt(f2(T["TT"]), pF[:, 0:GW], f2(R), ALU.add)
            yield
            psU, psW = B[3], B[0]
            for m in range(G):
                tile = gi * G + m
                kb.mm(psU[:, m * 64:(m + 1) * 64], T["TT"][:, m, :], VB[:, tile, :], start=True, stop=True)
            for m in range(G):
                tile = gi * G + m
                kb.mm(psW[0:64, m * 128:(m + 1) * 128], KBG[:, tile, :], T["TT"][:, m, :], start=True, stop=True)
            yield
            kb.acopy(f2(U[dr][:, gi * G:(gi + 1) * G, :]), psU[:, 0:G * 64])
            kb.copy(WT[sb_:sb_ + 64, tsl], psW[0:64, 0:GW])

        for gi in range(NG):
            gens = [titer(0, gi), titer(1, gi)]
            alive = [True, True]
            while any(alive):
                for i in range(2):
                    if alive[i]:
                        try:
                            next(gens[i])
                        except StopIteration:
                            alive[i] = False
        if hp == 0 and hh == 0:
            self.dump("AT0", AT[0][:, :, :])
            self.dump("AT1", AT[1][:, :, :])
            self.dump("U0", U[0][:, :, :])
            self.dump("U1", U[1][:, :, :])
            self.dump("WT", WT[:, :])
            self.dump("QGT", QGT[:, :])
            self.dump("KD0", KD[0][:, :, :])
            self.dump("DECB", DECB[:, :, :])
            self.dump("TT", tb["TT"][:, :, :])
            self.dump("NN", tb["NN"][:, :, :])
        if self.cstop == "tstage":
            return
        kb.memset(SF[:, :], 0.0)
        kb.memset(SBb[:, :], 0.0)
        for s in range(32):
            for dr in range(2):
                c = s if dr == 0 else 31 - s
                m = c // 2
                pb = (c % 2) * 64
                sb_ = dr * 64
                bW, bO, bS = self.PS[dr * 4], self.PS[dr * 4 + 1], self.PS[dr * 4 + 2]
                vn = VN[dr * 2 + s % 2]
                csl = slice(c * 64, (c + 1) * 64)
                kb.mm(bW[0:64, 0:64], WT[sb_:sb_ + 64, csl], SBb[sb_:sb_ + 64, :], start=True, stop=True)
                kb.tt(vn[pb:pb + 64, :], U[dr][pb:pb + 64, m, :], bW[0:64, 0:64], ALU.subtract)
                kb.mm(bO[0:64, 0:64], QGT[sb_:sb_ + 64, csl], SBb[sb_:sb_ + 64, :], start=True, stop=False, inc=False)
                kb.mm(bO[0:64, 0:64], AT[dr][pb:pb + 64, m, pb:pb + 64], vn[pb:pb + 64, :], start=False, stop=True)
                kb.acopy(OF[dr][pb:pb + 64, m, :], bO[0:64, 0:64])
                kb.mm(bS[0:64, 0:64], KD[dr][pb:pb + 64, m, :], vn[pb:pb + 64, :], start=True, stop=True)
                kb.stt(SF[sb_:sb_ + 64, :], SF[sb_:sb_ + 64, :], DECB[sb_:sb_ + 64, dr, c:c + 1], bS[0:64, 0:64],
                       ALU.mult, ALU.add)
                kb.acopy(SBb[sb_:sb_ + 64, :], SF[sb_:sb_ + 64, :])
        if hp == 0 and hh == 0:
            self.dump("OF0", OF[0][:, :, :])
            self.dump("OF1", OF[1][:, :, :])
        if self.cstop == "scan":
            return
        OS = OF[0]
        kb.tt(OS[:, :, :], OF[0][:, :, :], OF[1][:, :, :], ALU.add)
        kb.tt(OF[1][:, :, :], OS[:, :, :], OS[:, :, :], ALU.mult)
        MS = self.aalloc([128, 16], F32)
        kb.op(kb.dve, lambda: nc.vector.tensor_reduce(out=MS[:, :], in_=OF[1][:, :, :], axis=AX.X, op=ALU.add),
              reads=[OF[1][:, :, :]], writes=[MS[:, :]])
        self.rsqrt(MS[:, :], MS[:, :], 1.0 / 64)
        kb.tt(ON[:, :, bq:bq + 64], OS[:, :, :], self.bcast_last(MS[:, :].rearrange("p (a o) -> p a o", o=1), 64), ALU.mult)


def prepare(inputs):
    cols = make_cols(inputs)
    shared = {"cols": cols.array()}
    for nm in ("ffn1", "ffn2"):
        shared[nm + "_wg"] = np.ascontiguousarray(inputs[nm + "_w_gate"], dtype=np.float32)
        shared[nm + "_wu"] = np.ascontiguousarray(inputs[nm + "_w_up"], dtype=np.float32)
        shared[nm + "_wd"] = np.ascontiguousarray(inputs[nm + "_w_down"], dtype=np.float32)
    shared["win"] = np.ascontiguousarray(np.asarray(inputs["w_in"], np.float32)[:, :, win_perm()])
    shared["wout"] = np.ascontiguousarray(inputs["w_out"], dtype=np.float32)
    shared["relb"] = np.ascontiguousarray(inputs["rel_bias"], dtype=np.float32)
    shared.update(make_consts())
    return cols, shared


def run(inputs, cores=8, stop_after=None, parts="ABC", debug=False, cstop=None, ret_all=False):
    inputs = {k: np.asarray(v) for k, v in inputs.items()}
    cols, shared = prepare(inputs)
    prog = Prog(len(cols.cols), cols.idx, stop_after=stop_after, parts=parts)
    prog.debug = debug
    prog.cstop = cstop
    nc = prog.build()
    x = inputs["x"]
    in_maps = []
    for c in range(cores):
        m = dict(shared)
        m["xT"] = np.ascontiguousarray(x[c].T)
        in_maps.append(m)
    res = run_bass_kernel_spmd(nc, in_maps, core_ids=list(range(cores)))
    out = np.stack([np.ascontiguousarray(res.results[c]["yT"].T) for c in range(cores)], axis=0)
    if ret_all:
        return out.astype(np.float32), res.results
    return out.astype(np.float32)


def kernel(**inputs):
    return run(inputs, cores=8)
```

```python
import contextlib
import math
import numpy as np
import concourse.bass as bass
import concourse.mybir as mybir
from concourse.bass_utils import run_bass_kernel_spmd

F32 = mybir.dt.float32
BF16 = mybir.dt.bfloat16
AF = mybir.ActivationFunctionType
ALU = mybir.AluOpType
AX = mybir.AxisListType

S = 2048
D = 1024
DFF = 2816
NL = 2
NF = DFF // 128
EPS = 1e-6
GROUPS = ((0, 8), (8, 16), (16, 22))


class Sem:
    def __init__(self, h, name):
        self.h = h
        self.cnt = 0
        self.name = name


class Eng:
    def __init__(self, name, h, sem, selfsync):
        self.name = name
        self.h = h
        self.sem = sem
        self.selfsync = selfsync
        self.known = {}


def _box(ap):
    t = ap.tensor
    name = t.name
    space = str(ap.space)
    if "SB" not in space and "PSUM" not in space:
        return (name, 0, 1, 0, 1)
    dims = ap.ap
    shape = t.shape
    row = 1
    for s in list(shape)[1:]:
        row *= int(s)
    off = int(ap.offset)
    p0 = off // row
    f0 = off % row
    pstep, pcnt = dims[0]
    ext = 0
    for st, cn in dims[1:]:
        ext += (cn - 1) * abs(st)
    if pstep == 0:
        pcnt = 1
    sz = mybir.dt.size(ap.dtype)
    if "PSUM" in space:
        return (name, 0, 128, 0, 2048)
    return (name, p0, p0 + pcnt, f0 * sz, (f0 + ext + 1) * sz)


BK = 2048


def _bks(b):
    return range(b[3] // BK, (b[4] - 1) // BK + 1)


class KB:
    def __init__(self, nc, es):
        self.nc = nc
        self.es = es
        self.recs = {}
        self.notrack = set()
        self.pe = Eng("pe", nc.tensor, self.newsem("s_pe"), False)
        self.act = Eng("act", nc.scalar, self.newsem("s_act"), True)
        self.dve = Eng("dve", nc.vector, self.newsem("s_dve"), True)
        self.pool = Eng("pool", nc.gpsimd, self.newsem("s_pool"), True)
        self.sp = Eng("sp", nc.sync, self.newsem("s_sp"), True)
        self.dsems = {}
        self.nwaits = 0
        self.nops = 0

    def newsem(self, name):
        return Sem(self.es.enter_context(self.nc.semaphore(name)), name)

    def dsem(self, *key):
        if key not in self.dsems:
            self.dsems[key] = self.newsem("d_" + "_".join(str(k) for k in key))
        return self.dsems[key]

    def sb(self, name, shape, dt):
        return self.es.enter_context(self.nc.sbuf_tensor(name, list(shape), dt))

    def ps(self, name, shape, dt):
        return self.es.enter_context(self.nc.psum_tensor(name, list(shape), dt))

    def _collect(self, rb, wb):
        need = {}

        def add(s, v):
            if need.get(s, 0) < v:
                need[s] = v

        for b in rb:
            for k in _bks(b):
                for r in self.recs.get((b[0], k), ()):
                    if r[5] == "w" and r[1] < b[2] and b[1] < r[2] and r[3] < b[4] and b[3] < r[4]:
                        add(r[6], r[7])
        for b in wb:
            for k in _bks(b):
                for r in self.recs.get((b[0], k), ()):
                    if r[1] < b[2] and b[1] < r[2] and r[3] < b[4] and b[3] < r[4]:
                        add(r[6], r[7])
        return need

    def _commit(self, rb, wb, sem, val):
        for b in wb:
            rec = [b[0], b[1], b[2], b[3], b[4], "w", sem, val]
            for k in _bks(b):
                lst = self.recs.setdefault((b[0], k), [])
                lst[:] = [r for r in lst if not (b[1] <= r[1] and r[2] <= b[2] and b[3] <= r[3] and r[4] <= b[4])]
                lst.append(rec)
        for b in rb:
            rec = None
            for k in _bks(b):
                lst = self.recs.setdefault((b[0], k), [])
                for r in lst:
                    if r[5] == "r" and r[6] is sem and r[1] == b[1] and r[2] == b[2] and r[3] == b[3] and r[4] == b[4]:
                        r[7] = max(r[7], val)
                        break
                else:
                    if rec is None:
                        rec = [b[0], b[1], b[2], b[3], b[4], "r", sem, val]
                    lst.append(rec)

    def _boxes(self, aps):
        out = []
        for a in aps:
            b = _box(a)
            if b[0] in self.notrack:
                continue
            out.append(b)
        return out

    def _waits(self, eng, need):
        for s, v in need.items():
            if v <= 0:
                continue
            if s is eng.sem and not eng.selfsync:
                continue
            if eng.known.get(s, 0) >= v:
                continue
            eng.h.wait_ge(s.h, v)
            eng.known[s] = v
            self.nwaits += 1

    def op(self, eng, fn, reads=(), writes=(), inc=True):
        rb = self._boxes(reads)
        wb = self._boxes(writes)
        if eng is not self.pe:
            wb = wb + [b for b in rb if b[0].startswith("ps")]
            rb = [b for b in rb if not b[0].startswith("ps")]
        need = self._collect(rb, wb)
        self._waits(eng, need)
        ins = fn()
        self.nops += 1
        if inc:
            eng.sem.cnt += 1
            ins.then_inc(eng.sem.h, 1)
            val = eng.sem.cnt
        else:
            val = eng.sem.cnt + 1
        self._commit(rb, wb, eng.sem, val)
        return ins

    def dma(self, q, out, in_, dsem):
        rb = self._boxes([in_])
        wb = self._boxes([out])
        need = self._collect(rb, wb)
        if dsem.cnt > 0:
            need[dsem] = max(need.get(dsem, 0), dsem.cnt)
        self._waits(q, need)
        ins = q.h.dma_start(out=out, in_=in_)
        ins.then_inc(dsem.h, 16)
        dsem.cnt += 16
        self.nops += 1
        self._commit(rb, wb, dsem, dsem.cnt)

    def mm(self, out, lhsT, rhs, start, stop, inc=None):
        if inc is None:
            inc = stop
        return self.op(self.pe, lambda: self.nc.tensor.matmul(out, lhsT=lhsT, rhs=rhs, start=start, stop=stop),
                       reads=[lhsT, rhs], writes=[out], inc=inc)

    def activation(self, out, in_, func, bias=None, scale=None, extra_reads=()):
        kw = {}
        rd = [in_] + list(extra_reads)
        if bias is not None:
            kw["bias"] = bias
            if not isinstance(bias, (int, float)):
                rd.append(bias)
        if scale is not None:
            kw["scale"] = scale
            if not isinstance(scale, (int, float)):
                rd.append(scale)
        return self.op(self.act, lambda: self.nc.scalar.activation(out=out, in_=in_, func=func, **kw),
                       reads=rd, writes=[out])

    def tt(self, out, in0, in1, op, eng=None):
        eng = eng or self.dve
        return self.op(eng, lambda: eng.h.tensor_tensor(out=out, in0=in0, in1=in1, op=op),
                       reads=[in0, in1], writes=[out])

    def ts(self, out, in0, s1, s2, op0, op1=None, eng=None):
        eng = eng or self.dve
        rd = [in0]
        if not isinstance(s1, (int, float)):
            rd.append(s1)
        if s2 is not None and not isinstance(s2, (int, float)):
            rd.append(s2)
        if op1 is None:
            return self.op(eng, lambda: eng.h.tensor_scalar(out=out, in0=in0, scalar1=s1, scalar2=None, op0=op0),
                           reads=rd, writes=[out])
        return self.op(eng, lambda: eng.h.tensor_scalar(out=out, in0=in0, scalar1=s1, scalar2=s2, op0=op0, op1=op1),
                       reads=rd, writes=[out])

    def stt(self, out, in0, scalar, in1, op0, op1, eng=None):
        eng = eng or self.dve
        rd = [in0, in1]
        if not isinstance(scalar, (int, float)):
            rd.append(scalar)
        return self.op(eng, lambda: eng.h.scalar_tensor_tensor(out=out, in0=in0, scalar=scalar, in1=in1, op0=op0, op1=op1),
                       reads=rd, writes=[out])

    def copy(self, out, in_, eng=None):
        eng = eng or self.dve
        return self.op(eng, lambda: eng.h.tensor_copy(out=out, in_=in_), reads=[in_], writes=[out])

    def acopy(self, out, in_):
        return self.op(self.act, lambda: self.nc.scalar.copy(out=out, in_=in_), reads=[in_], writes=[out])

    def memset(self, ap, val, eng=None):
        eng = eng or self.dve
        return self.op(eng, lambda: eng.h.memset(ap, val), reads=[], writes=[ap])


class Cols:
    def __init__(self):
        self.cols = []
        self.idx = {}

    def add(self, key, vec):
        v = np.zeros(128, np.float32)
        vec = np.asarray(vec, np.float32).reshape(-1)
        v[: vec.shape[0]] = vec
        self.idx[key] = len(self.cols)
        self.cols.append(v)

    def add_chunks(self, key, vec):
        vec = np.asarray(vec, np.float32).reshape(-1, 128)
        for i in range(vec.shape[0]):
            self.add((key, i), vec[i])

    def array(self):
        return np.ascontiguousarray(np.stack(self.cols, axis=1))


def _swap64(v):
    v = np.asarray(v)
    return np.concatenate([v[32:64], v[0:32]])


def make_cols(inp):
    c = Cols()
    for l in range(NL):
        c.add_chunks(("ffn1_norm", l), inp["ffn1_norm"][l])
        c.add_chunks(("mix_norm", l), inp["mix_norm"][l])
        c.add_chunks(("ffn2_norm", l), inp["ffn2_norm"][l])
        for nm in ("a_q_norm", "a_k_norm", "b_q_norm", "b_k_norm", "c_out_norm"):
            g = inp[nm][l]
            c.add((nm, l), np.concatenate([g, g]))
            c.add((nm + "_sw", l), np.concatenate([_swap64(g), _swap64(g)]))
        cw = inp["c_conv"][l]
        for ch in range(9):
            for j in range(5):
                c.add(("conv", l, ch, j), cw[j, ch * 128:(ch + 1) * 128])
        c.add(("alog", l), inp["c_A_log"][l].reshape(-1))
        c.add(("dtb", l), inp["c_dt_bias"][l].reshape(-1))
    c.add("mF", [1.0] * 6 + [0.0] * 6)
    c.add("mB", [0.0] * 6 + [1.0] * 6)
    c.add("one", np.ones(128))
    c.add("eps", np.full(128, EPS))
    return c


NWC = 28 * 128 + 24


def win_perm():
    aq, ak, av, bq, bk, bv, cq, ck, cv, cz, cb, ca = 0, 256, 384, 512, 896, 1280, 1664, 2048, 2432, 2816, 3200, 3212

    def head(base, h):
        return list(range(base + h * 64, base + h * 64 + 64))

    def swp(cols):
        return cols[32:64] + cols[0:32]

    p = []
    qa0 = head(aq, 0) + head(aq, 2)
    qa1 = head(aq, 1) + head(aq, 3)
    p += qa0 + qa1
    p += swp(head(aq, 0)) + swp(head(aq, 2)) + swp(head(aq, 1)) + swp(head(aq, 3))
    ka = head(ak, 0) + head(ak, 1)
    p += ka + swp(head(ak, 0)) + swp(head(ak, 1))
    p += list(range(bq, bq + 384)) + list(range(bk, bk + 384))
    p += list(range(cq, cq + 384)) + list(range(ck, ck + 384)) + list(range(cv, cv + 384)) + list(range(cz, cz + 384))
    p += list(range(av, av + 128)) + list(range(bv, bv + 384))
    p += list(range(cb, cb + 12)) + list(range(ca, ca + 12))
    assert len(p) == NWC
    return np.array(p)


LW = 3072
R0 = 1535
TSW = 2944


def t5_bucket_np(rel):
    rel = np.asarray(rel, np.int64)
    half, exact = 16, 8
    sign = np.where(rel > 0, half, 0)
    n = np.abs(rel)
    nf = np.maximum(n, 1).astype(np.float32)
    large = exact + (np.log(nf / np.float32(exact)) / np.float32(math.log(1024 / exact)) * np.float32(half - exact)).astype(np.int32)
    large = np.minimum(large, half - 1)
    return sign + np.where(n < exact, n, large)


def make_consts():
    c = {}
    I = np.eye(128, dtype=np.float32)
    ii = np.arange(128)
    same64 = (ii[:, None] // 64) == (ii[None, :] // 64)
    same32 = (ii[:, None] // 32) == (ii[None, :] // 32)
    row = ii[:, None]
    col = ii[None, :]
    NEG = -30000.0
    blocks = [
        I,
        same64.astype(np.float32),
        I[::-1].copy(),
        same32.astype(np.float32),
        np.where(same64 & (col < row), 0.0, NEG),
        np.where(same64 & (col > row), 0.0, NEG),
        np.where(same64 & (row <= col), 0.0, NEG),
        np.where(same64 & (row >= col), 0.0, NEG),
        (~same32).astype(np.float32),
    ]
    c["cm"] = np.ascontiguousarray(np.concatenate([b.astype(np.float32) for b in blocks], axis=1))
    sel = np.zeros((12, 12, 128), np.float32)
    for r in range(12):
        sel[r, r, :] = 1.0
    c["sel"] = sel.reshape(12, 12 * 128)
    t = np.arange(S)
    rw = (t // 64).astype(np.float32)
    cl = (t % 64).astype(np.float32)
    inv = (10000.0 ** (-np.arange(16, dtype=np.float32) / 16)).astype(np.float32)
    ang = np.concatenate([rw[:, None] * inv, cl[:, None] * inv], axis=-1).astype(np.float32)
    cos = np.cos(ang).astype(np.float32).T
    sin = np.sin(ang).astype(np.float32).T
    COS = np.concatenate([cos, cos, cos, cos], axis=0)
    SIN = np.concatenate([-sin, sin, -sin, sin], axis=0)
    c["rope"] = np.ascontiguousarray(np.stack([COS, SIN], axis=0).astype(np.float32))
    i = np.arange(LW)
    r = R0 - i
    bkt = t5_bucket_np(r)
    oh = np.zeros((32, LW), np.float32)
    oh[bkt, i] = 1.0
    mult = ((np.abs(r) <= 64).astype(np.float32) + ((r % 4 == 0) & (np.abs(r) <= 256)).astype(np.float32)
            + ((r % 16 == 0) & (np.abs(r) <= 1024)).astype(np.float32))
    c["ohrev"] = oh
    c["multrev"] = np.ascontiguousarray(np.tile(mult[None, :], (6, 1)).astype(np.float32))
    return c


ARN = 50048


class Prog:
    def __init__(self, ncols, colidx, stop_after=None, parts="ABC"):
        self.colidx = colidx
        self.stop_after = stop_after
        self.parts = parts
        self.dbg_sems = []
        self.debug = False
        self.cstop = None
        nc = bass.Bass("TRN2", target_bir_lowering=False)
        self.nc = nc
        self.es = contextlib.ExitStack()
        d = {}

        def inp(name, shape, dt=F32):
            d[name] = nc.dram_tensor(name, list(shape), dt, kind="ExternalInput").ap()

        inp("xT", [D, S])
        inp("cols", [128, ncols])
        for nm in ("ffn1", "ffn2"):
            inp(nm + "_wg", [NL, D, DFF])
            inp(nm + "_wu", [NL, D, DFF])
            inp(nm + "_wd", [NL, DFF, D])
        inp("win", [NL, D, NWC])
        inp("wout", [NL, D, D])
        inp("relb", [32, 6])
        inp("cm", [128, 9 * 128])
        inp("sel", [12, 12 * 128])
        inp("rope", [2, 128, S])
        inp("ohrev", [32, LW])
        inp("multrev", [6, LW])
        self.inputs = dict(d)
        d["yT"] = nc.dram_tensor("yT", [D, S], F32, kind="ExternalOutput").ap()
        d["wrev"] = nc.dram_tensor("wrev", [6, LW], BF16, kind="Internal").ap()
        d["strips"] = nc.dram_tensor("strips", [6, 128, TSW], BF16, kind="Internal").ap()
        self.d = d
        self.ncols = ncols

    def col(self, key, n=128):
        i = self.colidx[key]
        return self.COLS[0:n, i:i + 1]

    def build(self):
        nc = self.nc
        with self.es:
            kb = KB(nc, self.es)
            self.kb = kb
            for k in self.inputs:
                kb.notrack.add(self.inputs[k].tensor.name)
            self.alloc()
            self.body()
        return nc

    def areset(self, base=0):
        self.aptr = base

    def aalloc(self, shape, dt):
        n = 1
        for x in shape[1:]:
            n *= x
        ne = n * (2 if dt == F32 else 1)
        off = self.aptr
        off += off % 2
        assert off + ne <= ARN, ("arena overflow", off, ne)
        self.aptr = off + ne
        v = self.AR[:, off:off + ne]
        if dt == F32:
            v = v.bitcast(F32)
        v = v[0:shape[0]]
        if len(shape) == 3:
            v = v.rearrange("p (a b) -> p a b", a=shape[1])
        return v

    def alloc(self):
        kb = self.kb
        self.H = kb.sb("H", [128, 8, S], F32)
        self.HN = kb.sb("HN", [128, 8, S], BF16)
        self.COLS = kb.sb("COLS", [128, self.ncols], F32)
        self.CMF = kb.sb("CMF", [128, 9, 128], F32)
        self.CMB = kb.sb("CMB", [128, 9, 128], BF16)
        self.SELF = kb.sb("SELF", [12, 12, 128], F32)
        self.ONES = kb.sb("ONES", [128, 128], BF16)
        self.AR = kb.sb("AR", [128, ARN], BF16)
        self.PS = [kb.ps("ps%d" % i, [128, 512], F32) for i in range(8)]

    def ffn_alloc(self):
        self.areset()
        self.WG = [self.aalloc([128, 8, 256], BF16) for i in range(2)]
        self.WU = [self.aalloc([128, 8, 256], BF16) for i in range(2)]
        self.WD = [self.aalloc([128, D], BF16) for i in range(8)]
        self.HID = self.aalloc([128, 8, S], BF16)
        self.SG = [self.aalloc([128, 512], F32) for i in range(4)]
        self.SQ = self.aalloc([128, 8, 512], BF16)
        self.RSTD = self.aalloc([128, 512], F32)

    def done(self, tag):
        return self.stop_after == tag

    def dump(self, name, ap):
        if not getattr(self, "debug", False):
            return
        t = self.nc.dram_tensor("dbg_" + name, list(ap.shape), F32, kind="ExternalOutput").ap()
        sm = self.kb.dsem("dbg", name)
        self.kb.dma(self.kb.pool, t, ap, sm)
        self.dbg_sems.append(sm)

    def body(self):
        kb = self.kb
        nc = self.nc
        d = self.d
        kb.dma(kb.sp, self.COLS[:, :], d["cols"], kb.dsem("cols"))
        kb.dma(kb.sp, self.CMF[:, :, :], d["cm"].rearrange("p (a b) -> p a b", a=9), kb.dsem("cm"))
        kb.dma(kb.sp, self.SELF[:, :, :], d["sel"].rearrange("p (a b) -> p a b", a=12), kb.dsem("sel"))
        xv = d["xT"].rearrange("(dc p) t -> p dc t", p=128)
        for dc in range(8):
            kb.dma(kb.sp, self.H[:, dc, :], xv[:, dc, :], kb.dsem("x", dc % 4))
        kb.memset(self.ONES[:, :], 1.0)
        kb.copy(self.CMB[:, :, :], self.CMF[:, :, :])
        self.IDB = self.CMB[:, 0, :]
        self.BD64B = self.CMB[:, 1, :]
        if "B" in self.parts:
            self.build_strips()
        for l in range(NL):
            self.ffn_alloc()
            if not getattr(self, "skip_ffn", False):
                self.rmsnorm(("ffn1_norm", l))
                self.ffn(d["ffn1_wg"][l], d["ffn1_wu"][l], d["ffn1_wd"][l])
            if self.done(("ffn1", l)):
                break
            self.rmsnorm(("mix_norm", l))
            self.mixer(l)
            if self.done(("mix", l)):
                break
            self.ffn_alloc()
            self.rmsnorm(("ffn2_norm", l))
            self.ffn(d["ffn2_wg"][l], d["ffn2_wu"][l], d["ffn2_wd"][l])
            if self.done(("ffn2", l)):
                break
        yv = d["yT"].rearrange("(dc p) t -> p dc t", p=128)
        osems = []
        for dc in range(8):
            sm = kb.dsem("y", dc)
            kb.dma(kb.sp, yv[:, dc, :], self.H[:, dc, :], sm)
            osems.append(sm)
        for sm in osems + self.dbg_sems:
            nc.sync.wait_ge(sm.h, sm.cnt)

    def rsqrt(self, out, in_, scale):
        kb = self.kb
        n = out.shape[0]
        kb.activation(out, in_, AF.Sqrt, bias=self.col("eps", n), scale=scale)
        kb.op(kb.dve, lambda: self.nc.vector.reciprocal(out=out, in_=out), reads=[out], writes=[out])

    def rmsnorm(self, gkey):
        kb = self.kb
        for tc in range(4):
            tsl = slice(tc * 512, (tc + 1) * 512)
            for dc in range(8):
                kb.activation(self.SQ[:, dc, :], self.H[:, dc, tsl], AF.Square)
            bank = self.PS[tc % 2]
            for dc in range(8):
                kb.mm(bank[:, :], self.ONES[:, :], self.SQ[:, dc, :], start=(dc == 0), stop=(dc == 7))
            self.rsqrt(self.RSTD[:, :], bank[:, :], 1.0 / D)
            for dc in range(8):
                kb.stt(self.HN[:, dc, tsl], self.H[:, dc, tsl], self.col((gkey, dc)), self.RSTD[:, :],
                       ALU.mult, ALU.mult)

    def ffn(self, wg, wu, wd):
        kb = self.kb
        wgv = wg.rearrange("(kc p) f -> p kc f", p=128)
        wuv = wu.rearrange("(kc p) f -> p kc f", p=128)
        wdv = wd.rearrange("(f p) d -> p f d", p=128)

        def load_pair(p):
            s = p % 2
            kb.dma(kb.pool, self.WG[s][:, :, :], wgv[:, :, p * 256:(p + 1) * 256], kb.dsem("wg", s))
            kb.dma(kb.pool, self.WU[s][:, :, :], wuv[:, :, p * 256:(p + 1) * 256], kb.dsem("wu", s))

        load_pair(0)
        load_pair(1)
        for (f0, f1) in GROUPS:
            for i, f in enumerate(range(f0, f1)):
                kb.dma(kb.pool, self.WD[i][:, :], wdv[:, f, :], kb.dsem("wd", i))
            for f in range(f0, f1):
                p = f // 2
                s = p % 2
                csl = slice((f % 2) * 128, (f % 2) * 128 + 128)
                for (W, b0) in ((self.WG[s], 0), (self.WU[s], 4)):
                    for kc in range(8):
                        for tc in range(4):
                            kb.mm(self.PS[b0 + tc][:, :], W[:, kc, csl], self.HN[:, kc, tc * 512:(tc + 1) * 512],
                                  start=(kc == 0), stop=(kc == 7))
                if f % 2 == 1 and p + 2 < NF // 2:
                    load_pair(p + 2)
                for tc in range(4):
                    kb.activation(self.SG[tc][:, :], self.PS[tc][:, :], AF.Silu)
                    kb.tt(self.HID[:, f - f0, tc * 512:(tc + 1) * 512], self.SG[tc][:, :], self.PS[4 + tc][:, :],
                          ALU.mult)
            nfl = f1 - f0
            for dc in range(8):
                b0 = (dc % 2) * 4
                for fl in range(nfl):
                    for tc in range(4):
                        kb.mm(self.PS[b0 + tc][:, :], self.WD[fl][:, dc * 128:(dc + 1) * 128],
                              self.HID[:, fl, tc * 512:(tc + 1) * 512], start=(fl == 0), stop=(fl == nfl - 1))
                for tc in range(4):
                    hs = self.H[:, dc, tc * 512:(tc + 1) * 512]
                    kb.stt(hs, self.PS[b0 + tc][:, :], 0.5, hs, ALU.mult, ALU.add)

    def load_win(self, l, dst, c0, ncol, key):
        src = self.d["win"][l].rearrange("(kc p) c -> p kc c", p=128)[:, :, c0:c0 + ncol]
        self.kb.dma(self.kb.pool, dst, src, self.kb.dsem("win", key))

    def proj(self, bank, W, c0, tc, ncol=128):
        kb = self.kb
        for kc in range(8):
            kb.mm(bank[0:ncol, :], W[:, kc, c0:c0 + ncol], self.HN[:, kc, tc * 512:(tc + 1) * 512],
                  start=(kc == 0), stop=(kc == 7))

    def wout(self, l, mcs, srcs):
        kb = self.kb
        wo = []
        for i, mc in enumerate(mcs):
            w = self.aalloc([128, D], BF16)
            kb.dma(kb.pool, w[:, :], self.d["wout"][l][mc * 128:(mc + 1) * 128, :], kb.dsem("wo", i))
            wo.append(w)
        n = len(mcs)
        for dc in range(8):
            b0 = (dc % 2) * 4
            for i in range(n):
                for tc in range(4):
                    kb.mm(self.PS[b0 + tc][:, :], wo[i][:, dc * 128:(dc + 1) * 128],
                          srcs[i][:, tc * 512:(tc + 1) * 512], start=(i == 0), stop=(i == n - 1))
            for tc in range(4):
                hs = self.H[:, dc, tc * 512:(tc + 1) * 512]
                kb.tt(hs, self.PS[b0 + tc][:, :], hs, ALU.add)

    def seg2(self, base_ap, delta):
        a = base_ap.ap
        return bass.AP(base_ap.tensor, base_ap.offset, [[a[0][0], a[0][1]], [delta, 2], [1, 64]])

    def mixer(self, l):
        self.areset()
        if "A" in self.parts or "B" in self.parts:
            self.vtok(l)
        base = self.aptr
        if "A" in self.parts:
            self.areset(base)
            self.mixer_a(l)
        if "B" in self.parts:
            self.areset(base)
            self.mixer_b(l)
        if "C" in self.parts:
            self.areset()
            self.mixer_c(l)

    def vtok(self, l):
        kb = self.kb
        self.VT = self.aalloc([128, 16, 1024], BF16)
        mark = self.aptr
        WV = self.aalloc([128, 8, 512], BF16)
        self.load_win(l, WV[:, :, :], 24 * 128, 512, "wv")
        kb.memset(self.VT.rearrange("p m (s c) -> p (m s) c", s=8)[:, :, 64:128], 1.0)
        for m in range(16):
            bank = self.PS[m % 2]
            for kc in range(8):
                kb.mm(bank[:, :], self.HN[:, kc, m * 128:(m + 1) * 128], WV[:, kc, :], start=(kc == 0), stop=(kc == 7))
            vdst = self.VT[:, m, :].rearrange("p (s c) -> p s c", s=8)[:, :, 0:64]
            vsrc = bank[:, :].rearrange("p (s c) -> p s c", s=8)
            if m % 2 == 0:
                kb.copy(vdst, vsrc)
            else:
                kb.acopy(vdst, vsrc)
        self.aptr = mark + 0

    def attention(self, qsrc, ksrc, vcol, dst, strip=None):
        kb = self.kb
        for qc in range(4):
            qsl = slice(qc * 512, (qc + 1) * 512)
            psO = self.PS[4 + qc % 2]
            kts = []
            for kt in range(16):
                dk = kt * 128 - qc * 512
                if strip is not None and not (-1024 <= dk <= 1408):
                    continue
                kts.append(kt)
            NB = 4
            LA = 3

            def emit_qk(n):
                kt = kts[n]
                psS = self.PS[n % NB]
                pt = self.PT[n % NB]
                kb.mm(psS[:, :], ksrc[:, kt * 128:(kt + 1) * 128], qsrc[:, qsl], start=True, stop=True)
                kb.activation(pt[:, :], psS[:, :], AF.Exp, scale=0.125)
                if strip is not None:
                    cs = 1408 - (kt * 128 - qc * 512)
                    kb.tt(pt[:, :], pt[:, :], strip[:, cs:cs + 512], ALU.mult)

            def emit_pv(n):
                kt = kts[n]
                vl = self.VT[:, kt, vcol * 2:vcol * 2 + 128]
                kb.mm(psO[:, :], vl, self.PT[n % NB][:, :], start=(n == 0), stop=(n == len(kts) - 1))

            for n in range(len(kts) + LA):
                if n < len(kts):
                    emit_qk(n)
                if n >= LA:
                    emit_pv(n - LA)
            kb.op(kb.dve, lambda: self.nc.vector.reciprocal(out=self.REC[0:64, :], in_=psO[64:128, :]),
                  reads=[psO[64:128, :]], writes=[self.REC[0:64, :]])
            kb.tt(dst[:, qsl], psO[0:64, :], self.REC[0:64, :], ALU.mult)

    def qknorm_chunk(self, l, W, c0, cs0, gkey, dst, tc, rope, split=None):
        kb = self.kb
        tsl = slice(tc * 512, (tc + 1) * 512)
        px = self.PS[0]
        self.proj(px, W, c0, tc)
        kb.activation(self.SQT[:, :], px[:, :], AF.Square)
        pst = self.PS[2]
        kb.mm(pst[:, :], self.BD64B, self.SQT[:, :], start=True, stop=True)
        self.rsqrt(self.RST[:, :], pst[:, :], 1.0 / 64)
        if not rope:
            if split is None:
                kb.stt(dst[:, tsl], px[:, :], self.col((gkey, l)), self.RST[:, :], ALU.mult, ALU.mult)
            else:
                for hf in range(2):
                    ps_ = slice(hf * 64, hf * 64 + 64)
                    kb.stt(split[hf][ps_, tsl], px[ps_, :], self.col((gkey, l))[ps_, :], self.RST[ps_, :],
                           ALU.mult, ALU.mult)
            return
        pw = self.PS[1]
        self.proj(pw, W, cs0, tc)
        kb.stt(self.T0[:, :], px[:, :], self.col((gkey, l)), self.RST[:, :], ALU.mult, ALU.mult)
        kb.stt(self.T1[:, :], pw[:, :], self.col((gkey + "_sw", l)), self.RST[:, :], ALU.mult, ALU.mult)
        kb.tt(self.T0[:, :], self.T0[:, :], self.CS[:, 0, :], ALU.mult)
        kb.tt(self.T1[:, :], self.T1[:, :], self.CS[:, 1, :], ALU.mult)
        if split is None:
            kb.tt(dst[:, tsl], self.T0[:, :], self.T1[:, :], ALU.add)
        else:
            for hf in range(2):
                ps_ = slice(hf * 64, hf * 64 + 64)
                kb.tt(split[hf][ps_, tsl], self.T0[ps_, :], self.T1[ps_, :], ALU.add)

    def att_tmps(self):
        self.PT = [self.aalloc([128, 512], BF16) for i in range(4)]
        self.REC = self.aalloc([128, 512], F32)
        self.SQT = self.aalloc([128, 512], BF16)
        self.RST = self.aalloc([128, 512], F32)
        self.T0 = self.aalloc([128, 512], F32)
        self.T1 = self.aalloc([128, 512], F32)

    def mixer_a(self, l):
        kb = self.kb
        self.att_tmps()
        self.CS = self.aalloc([128, 2, 512], F32)
        WA = self.aalloc([128, 8, 768], BF16)
        self.load_win(l, WA[:, :, :], 0, 768, "wa")
        QZ = [[self.aalloc([128, S], BF16) for j in range(2)] for i in range(2)]
        KT = self.aalloc([128, S], BF16)
        OUT = [self.aalloc([128, S], BF16) for i in range(2)]
        for i in range(2):
            kb.memset(QZ[i][0][64:128, :], 0.0)
            kb.memset(QZ[i][1][0:64, :], 0.0)
        rv = self.d["rope"].rearrange("c p t -> p c t")
        for tc in range(4):
            kb.dma(kb.sp, self.CS[:, :, :], rv[:, :, tc * 512:(tc + 1) * 512], kb.dsem("rope"))
            self.qknorm_chunk(l, WA, 0, 256, "a_q_norm", None, tc, True, split=QZ[0])
            self.qknorm_chunk(l, WA, 128, 384, "a_q_norm", None, tc, True, split=QZ[1])
            self.qknorm_chunk(l, WA, 512, 640, "a_k_norm", KT, tc, True)
        for h in range(4):
            g = h // 2
            self.attention(QZ[h % 2][g], KT, g * 64, OUT[h // 2][(h % 2) * 64:(h % 2) * 64 + 64, :])
        self.wout(l, [0, 1], OUT)

    def build_strips(self):
        kb = self.kb
        self.areset()
        d = self.d
        OH = self.aalloc([32, LW], F32)
        MU = self.aalloc([6, LW], F32)
        RB = self.aalloc([32, 6], F32)
        W6 = self.aalloc([6, LW], F32)
        W6B = self.aalloc([6, LW], BF16)
        kb.dma(kb.sp, OH[:, :], d["ohrev"], kb.dsem("oh"))
        kb.dma(kb.sp, MU[:, :], d["multrev"], kb.dsem("mu"))
        kb.dma(kb.sp, RB[:, :], d["relb"], kb.dsem("rb"))
        for n in range(LW // 512):
            sl = slice(n * 512, (n + 1) * 512)
            bank = self.PS[n % 2]
            kb.mm(bank[0:6, :], RB[:, :], OH[:, sl], start=True, stop=True)
            kb.activation(W6[:, sl], bank[0:6, :], AF.Exp)
        kb.tt(W6B[:, :], W6[:, :], MU[:, :], ALU.mult)
        kb.dma(kb.sp, d["wrev"], W6B[:, :], kb.dsem("wrev"))
        HK = self.aalloc([128, TSW], BF16)
        TSB = self.aalloc([128, TSW], BF16)
        JB = self.CMB[:, 2, :]
        for h in range(6):
            src = bass.AP(d["wrev"].tensor, h * LW, [[1, 128], [1, TSW]])
            kb.dma(kb.sp, HK[:, :], src, kb.dsem("hk"))
            for n in range((TSW + 511) // 512):
                w = min(512, TSW - n * 512)
                sl = slice(n * 512, n * 512 + w)
                bank = self.PS[n % 2]
                kb.mm(bank[:, 0:w], JB, HK[:, sl], start=True, stop=True)
                if n % 2 == 0:
                    kb.copy(TSB[:, sl], bank[:, 0:w])
                else:
                    kb.acopy(TSB[:, sl], bank[:, 0:w])
            kb.dma(kb.sp, d["strips"][h], TSB[:, :], kb.dsem("strips"))

    def mixer_b(self, l):
        kb = self.kb
        self.att_tmps()
        TS = [self.aalloc([128, TSW], BF16) for i in range(2)]
        WB = [self.aalloc([128, 8, 256], BF16) for i in range(2)]
        QZ = [[self.aalloc([128, S], BF16) for j in range(2)] for i in range(2)]
        KBf = [self.aalloc([128, S], BF16) for i in range(2)]
        OUT1 = self.aalloc([128, S], BF16)
        OUT = [OUT1, OUT1]
        for i in range(2):
            kb.memset(QZ[i][0][64:128, :], 0.0)
            kb.memset(QZ[i][1][0:64, :], 0.0)
        mark = self.aptr
        for hp in range(3):
            s = hp % 2
            self.load_win(l, WB[s][:, :, 0:128], (6 + hp) * 128, 128, ("wbq", s))
            self.load_win(l, WB[s][:, :, 128:256], (9 + hp) * 128, 128, ("wbk", s))
            for tc in range(4):
                self.qknorm_chunk(l, WB[s], 0, None, "b_q_norm", None, tc, False, split=QZ[s])
                self.qknorm_chunk(l, WB[s], 128, None, "b_k_norm", KBf[s], tc, False)
            for hh in range(2):
                h = 2 * hp + hh
                kb.dma(kb.sp, TS[hh][:, :], self.d["strips"][h], kb.dsem("ts", hh))
                self.attention(QZ[s][hh], KBf[s], 128 + h * 64,
                               OUT[s][hh * 64:(hh + 1) * 64, :], strip=TS[hh])
            self.aptr = mark
            self.wout(l, [2 + hp], [OUT[s]])

    def bcast_mid(self, ap2d, n):
        a = ap2d.ap
        return bass.AP(ap2d.tensor, ap2d.offset, [[a[0][0], a[0][1]], [0, n], [a[1][0], a[1][1]]])

    def bcast_last(self, ap, n):
        a = [list(x) for x in ap.ap]
        a[-1] = [0, n]
        return bass.AP(ap.tensor, ap.offset, a)

    def c_gates(self, l):
        kb = self.kb
        nc = self.nc
        GW = self.aalloc([128, 8, 24], BF16)
        self.load_win(l, GW[:, :, :], 28 * 128, 24, "wg12")
        self.GC = self.aalloc([12, S], F32)
        self.EG = self.aalloc([12, S], F32)
        self.TOKT = self.aalloc([128, 5, 16 * 12], F32)
        self.TOT = self.aalloc([12, 32], F32)
        self.DEC = self.aalloc([12, 32], F32)
        NEGA = self.aalloc([12, 2], F32)
        mark = self.aptr
        B0 = self.aalloc([12, S], F32)
        B1 = self.aalloc([12, S], F32)
        B2 = self.aalloc([12, S], F32)
        B3 = self.aalloc([12, S], F32)
        B4 = self.aalloc([12, S], F32)
        kb.activation(NEGA[:, 0:1], self.col(("alog", l), 12), AF.Exp)
        kb.ts(NEGA[:, 1:2], NEGA[:, 0:1], -1.0, 0.0, ALU.mult, ALU.add)
        for tc in range(4):
            tsl = slice(tc * 512, (tc + 1) * 512)
            pb_, pa_ = self.PS[0], self.PS[1]
            self.proj(pb_, GW, 0, tc, ncol=12)
            self.proj(pa_, GW, 12, tc, ncol=12)
            kb.activation(B0[:, tsl], pb_[0:12, :], AF.Sigmoid)
            kb.activation(B4[:, tsl], pa_[0:12, :], AF.Exp, bias=self.col(("dtb", l), 12))
            kb.activation(B4[:, tsl], B4[:, tsl], AF.Ln, bias=self.col("one", 12))
            kb.ts(B1[:, tsl], B4[:, tsl], NEGA[:, 1:2], 0.0, ALU.mult, ALU.add)

        if self.cstop == "g2":
            return

        def v3(b):
            return b.rearrange("p (c t) -> p c t", c=32)

        src = B1
        seq = [B2, B3, B2, B3, B2, B3]
        for k, sh in enumerate((1, 2, 4, 8, 16, 32)):
            dst = seq[k]
            kb.tt(v3(dst)[:, :, sh:64], v3(src)[:, :, sh:64], v3(src)[:, :, 0:64 - sh], ALU.add)
            kb.copy(v3(dst)[:, :, 0:sh], v3(src)[:, :, 0:sh])
            src = dst
        P = B3
        if self.cstop == "g3":
            return
        kb.copy(self.TOT[:, :], v3(P)[:, :, 63])
        totb = self.bcast_last(self.TOT[:, :].rearrange("p (c o) -> p c o", o=1), 64)
        kb.tt(v3(B2), totb, v3(P), ALU.subtract)
        kb.tt(B2[:, :], B2[:, :], B1[:, :], ALU.add)
        kb.ts(B1[:, :], P[:, :], self.col("mF", 12), 0.0, ALU.mult, ALU.add)
        kb.stt(self.GC[:, :], B2[:, :], self.col("mB", 12), B1[:, :], ALU.mult, ALU.add)
        kb.activation(self.DEC[:, :], self.TOT[:, :], AF.Exp)
        kb.tt(v3(B1), totb, v3(self.GC), ALU.subtract)
        kb.activation(B1[:, :], B1[:, :], AF.Exp)
        kb.activation(B2[:, :], B0[:, :], AF.Ln)
        kb.tt(B2[:, :], B2[:, :], self.GC[:, :], ALU.add)
        kb.ts(B3[:, :], self.GC[:, :], -1.0, 0.0, ALU.mult, ALU.add)
        kb.activation(self.EG[:, :], self.GC[:, :], AF.Exp)
        kb.tt(B4[:, :], self.EG[:, :], B0[:, :], ALU.mult)
        if self.cstop == "g4":
            return
        IDF = self.CMF[0:12, 0, 0:12]
        for q, X in enumerate((B2, B3, B4, B0, B1)):
            bank = self.PS[2 + q % 2]
            for m in range(16):
                kb.mm(bank[:, m * 12:(m + 1) * 12], X[0:12, m * 128:(m + 1) * 128], IDF, start=True, stop=True)
            kb.copy(self.TOKT[:, q, :], bank[:, 0:192])
        self.aptr = mark

    def tok(self, q, tile, r):
        return self.TOKT[:, q, tile * 12 + r:tile * 12 + r + 1]

    def tokb(self, q, r):
        base = self.TOKT[:, q, r:r + 1]
        a = base.ap
        return bass.AP(base.tensor, base.offset, [[a[0][0], a[0][1]], [12, 16], [0, 64]])

    def mixer_c(self, l):
        kb = self.kb
        self.c_gates(l)
        if self.cstop in ("g2", "g3", "g4"):
            return
        self.dump("GC", self.GC[:, :])
        self.dump("TOKT", self.TOKT[:, :, :])
        self.dump("DEC", self.DEC[:, :])
        if self.cstop == "gates":
            return
        base_pair = self.aptr
        for hp in range(3):
            self.areset(base_pair)
            self.c_pair(l, hp)

    def c_pair(self, l, hp):
        kb = self.kb
        nc = self.nc
        QT = self.aalloc([128, S], BF16)
        KT = self.aalloc([128, S], BF16)
        KTOK = self.aalloc([128, 16, 128], BF16)
        VTOK = self.aalloc([128, 16, 128], BF16)
        ON = self.aalloc([128, 16, 128], BF16)
        self.SQT = self.aalloc([128, 512], BF16)
        self.RST = self.aalloc([128, 512], F32)
        mark = self.aptr
        WC = self.aalloc([128, 8, 384], BF16)
        XS = self.aalloc([128, S + 4], F32)
        ACC = self.aalloc([128, S], F32)
        VTf = self.aalloc([128, S], BF16)
        for ci in range(3):
            self.load_win(l, WC[:, :, ci * 128:(ci + 1) * 128], (12 + 3 * ci + hp) * 128, 128, ("wc", ci))
        kb.memset(XS[:, 0:2], 0.0)
        kb.memset(XS[:, S + 2:S + 4], 0.0)
        for ci, kind in enumerate("qkv"):
            for tc in range(4):
                bank = self.PS[tc % 2]
                self.proj(bank, WC, ci * 128, tc)
                if tc % 2 == 0:
                    kb.copy(XS[:, 2 + tc * 512:2 + (tc + 1) * 512], bank[:, :])
                else:
                    kb.acopy(XS[:, 2 + tc * 512:2 + (tc + 1) * 512], bank[:, :])
            ch = ci * 3 + hp
            kb.ts(ACC[:, :], XS[:, 0:S], self.col(("conv", l, ch, 0)), 0.0, ALU.mult, ALU.add)
            for j in range(1, 5):
                kb.stt(ACC[:, :], XS[:, j:j + S], self.col(("conv", l, ch, j)), ACC[:, :], ALU.mult, ALU.add)
            if kind == "v":
                kb.activation(VTf[:, :], ACC[:, :], AF.Silu)
                continue
            kb.activation(ACC[:, :], ACC[:, :], AF.Silu)
            dst = QT if kind == "q" else KT
            for tc in range(4):
                tsl = slice(tc * 512, (tc + 1) * 512)
                kb.activation(self.SQT[:, :], ACC[:, tsl], AF.Square)
                pst = self.PS[2 + tc % 2]
                kb.mm(pst[:, :], self.BD64B, self.SQT[:, :], start=True, stop=True)
                self.rsqrt(self.RST[:, :], pst[:, :], 1.0)
                kb.stt(dst[:, tsl], ACC[:, tsl], 0.125 if kind == "q" else 1.0, self.RST[:, :], ALU.mult, ALU.mult)
        for (src, dstk) in ((KT, KTOK), (VTf, VTOK)):
            for g4 in range(4):
                bank = self.PS[4 + g4 % 2]
                for m4 in range(4):
                    m = g4 * 4 + m4
                    kb.mm(bank[:, m4 * 128:(m4 + 1) * 128], src[:, m * 128:(m + 1) * 128], self.IDB, start=True, stop=True)
                if g4 % 2 == 0:
                    kb.copy(dstk[:, g4 * 4:(g4 + 1) * 4, :], bank[:, :].rearrange("p (a b) -> p a b", a=4))
                else:
                    kb.acopy(dstk[:, g4 * 4:(g4 + 1) * 4, :], bank[:, :].rearrange("p (a b) -> p a b", a=4))
        if hp == 0:
            self.dump("QT", QT[:, :])
            self.dump("KT", KT[:, :])
            self.dump("KTOK", KTOK[:, :, :])
            self.dump("VTOK", VTOK[:, :, :])
        if self.cstop == "conv":
            return
        for hh in range(2):
            self.areset(mark)
            if self.cstop in ("t1", "t1a", "t1b", "t2", "t3", "t4", "t5") and (hp, hh) != (0, 0):
                continue
            self.c_head(l, hp, hh, QT, KT, KTOK, VTOK, ON)
        if self.cstop is not None:
            return
        self.areset(mark)
        WZ = self.aalloc([128, 8, 128], BF16)
        OUTC = self.aalloc([128, S], BF16)
        SZ = self.aalloc([128, 512], F32)
        self.load_win(l, WZ[:, :, :], (21 + hp) * 128, 128, "wz")
        for tc in range(4):
            tsl = slice(tc * 512, (tc + 1) * 512)
            pz = self.PS[tc % 2]
            self.proj(pz, WZ, 0, tc)
            kb.activation(SZ[:, :], pz[:, :], AF.Silu)
            pt = self.PS[2 + tc % 2]
            for m4 in range(4):
                m = tc * 4 + m4
                kb.mm(pt[:, m4 * 128:(m4 + 1) * 128], ON[:, m, :], self.IDB, start=True, stop=True)
            kb.stt(OUTC[:, tsl], pt[:, :], self.col(("c_out_norm", l)), SZ[:, :], ALU.mult, ALU.mult)
        self.wout(l, [5 + hp], [OUTC])

    def c_head(self, l, hp, hh, QT, KT, KTOK, VTOK, ON):
        kb = self.kb
        nc = self.nc
        h = 2 * hp + hh
        bq = hh * 64
        AT = [self.aalloc([128, 16, 128], BF16) for i in range(2)]
        U = [self.aalloc([128, 16, 64], BF16) for i in range(2)]
        KD = [self.aalloc([128, 16, 64], BF16) for i in range(2)]
        WT = self.aalloc([128, S], BF16)
        QGT = self.aalloc([128, S], BF16)
        DECB = self.aalloc([128, 2, 32], F32)
        SF = self.aalloc([128, 64], F32)
        SBb = self.aalloc([128, 64], BF16)
        VN = [self.aalloc([128, 64], BF16) for i in range(4)]
        G = 2
        NG = 16 // G
        mark_t = self.aptr
        OF = [self.aalloc([128, 16, 64], F32) for i in range(2)]
        self.aptr = mark_t
        KBGs = [self.aalloc([128, 16, 64], BF16) for i in range(2)]
        VBs = [self.aalloc([128, 16, 64], BF16) for i in range(2)]
        names = ["DN", "DT", "NN", "NT", "Q0", "Q0P", "CN", "R0", "QA", "QPA", "QB", "QPB", "R1"]
        tbs = []
        XNs, XTs = [], []
        for dr in range(2):
            t = {n: self.aalloc([128, G, 128], BF16) for n in names}
            t["T0"], t["Z"], t["TT"] = t["QA"], t["QPA"], t["QB"]
            tbs.append(t)
            XNs.append(self.aalloc([128, G, 128], F32))
            XTs.append(self.aalloc([128, G, 128], F32))
        tb = tbs[1]

        def f2(t):
            return t.rearrange("p a b -> p (a b)")

        IDB = self.IDB
        GW = G * 128
        for dr in range(2):
            r = dr * 6 + h
            kb.tt(KBGs[dr][:, :, :], KTOK[:, :, bq:bq + 64], self.tokb(2, r), ALU.mult)
            kb.tt(VBs[dr][:, :, :], VTOK[:, :, bq:bq + 64], self.tokb(3, r), ALU.mult)
            kb.tt(KD[dr][:, :, :], KTOK[:, :, bq:bq + 64], self.tokb(4, r), ALU.mult)
            pdc = self.PS[7]
            kb.mm(pdc[:, 0:32], self.SELF[0:12, r, :], self.DEC[0:12, :], start=True, stop=True)
            kb.copy(DECB[:, dr, :], pdc[:, 0:32])

        def titer(dr, gi):
            r = dr * 6 + h
            sb_ = dr * 64
            T = tbs[dr]
            XN, XT = XNs[dr], XTs[dr]
            KBG, VB = KBGs[dr], VBs[dr]
            B = [self.PS[dr * 4 + i] for i in range(4)]
            tsl = slice(gi * GW, (gi + 1) * GW)

            def mmg(bank, lt, rt):
                for m in range(G):
                    kb.mm(bank[:, m * 128:(m + 1) * 128], lt[:, m, :], rt[:, m, :] if rt is not None else IDB,
                          start=True, stop=True)

            psG, psE, psK, psQ = B[0], B[1], B[2], B[3]
            kb.mm(psG[:, 0:GW], self.SELF[0:12, r, :], self.GC[0:12, tsl], start=True, stop=True)
            kb.mm(psE[:, 0:GW], self.SELF[0:12, r, :], self.EG[0:12, tsl], start=True, stop=True)
            for m in range(G):
                tk = slice((gi * G + m) * 128, (gi * G + m + 1) * 128)
                bl = slice(m * 128, (m + 1) * 128)
                kb.mm(psK[:, bl], KT[bq:bq + 64, tk], KT[bq:bq + 64, tk], start=True, stop=True)
            for m in range(G):
                tk = slice((gi * G + m) * 128, (gi * G + m + 1) * 128)
                bl = slice(m * 128, (m + 1) * 128)
                kb.mm(psQ[:, bl], KT[bq:bq + 64, tk], QT[bq:bq + 64, tk], start=True, stop=True)
            yield
            for m in range(G):
                kb.stt(XN[:, m, :], psG[:, m * 128:(m + 1) * 128], -1.0, self.CMF[:, 4 + dr, :], ALU.mult, ALU.add)
                kb.tt(XT[:, m, :], psG[:, m * 128:(m + 1) * 128], self.CMF[:, 6 + dr, :], ALU.add)
            kb.tt(QGT[sb_:sb_ + 64, tsl], QT[bq:bq + 64, tsl], psE[bq:bq + 64, 0:GW], ALU.mult)
            for m in range(G):
                tile = gi * G + m
                kb.activation(T["DN"][:, m, :], XN[:, m, :], AF.Exp, bias=self.tok(0, tile, r))
                kb.activation(T["DT"][:, m, :], XT[:, m, :], AF.Exp, bias=self.tok(1, tile, r))
            yield
            kb.stt(f2(T["NN"]), psK[:, 0:GW], -1.0, f2(T["DN"]), ALU.mult, ALU.mult)
            kb.tt(f2(AT[dr][:, gi * G:(gi + 1) * G, :]), psQ[:, 0:GW], f2(T["DT"]), ALU.mult)
            mmg(B[0], T["NN"], None)
            yield
            kb.acopy(f2(T["NT"]), B[0][:, 0:GW])
            pe = kb.pool
            kb.tt(T["Q0P"][:, :, :], T["NN"][:, :, :], self.bcast_mid(self.CMB[:, 3, :], G), ALU.mult, eng=pe)
            kb.tt(T["CN"][:, :, :], T["NN"][:, :, :], self.bcast_mid(self.CMB[:, 8, :], G), ALU.mult, eng=pe)
            yield
            kb.tt(T["Q0"][:, :, :], T["NT"][:, :, :], self.bcast_mid(self.CMB[:, 3, :], G), ALU.mult, eng=pe)
            kb.tt(T["R0"][:, :, :], T["Q0"][:, :, :], self.bcast_mid(self.CMB[:, 0, :], G), ALU.add, eng=pe)
            yield
            Q, QP, R = T["Q0"], T["Q0P"], T["R0"]
            alt = [(T["QA"], T["QPA"]), (T["QB"], T["QPB"])]
            for k in range(1, 5):
                Qn, QPn = alt[(k - 1) % 2]
                Rn = T["R1"] if k % 2 == 1 else T["R0"]
                pA, pB, pC = B[1], B[2], B[3]
                if k < 4:
                    mmg(pA, QP, Q)
                mmg(pB, Q, QP)
                yield
                if k < 4:
                    kb.acopy(f2(Qn), pA[:, 0:GW])
                if k % 2 == 0:
                    kb.acopy(f2(QPn), pB[:, 0:GW])
                else:
                    kb.copy(f2(QPn), pB[:, 0:GW])
                yield
                mmg(pC, QPn, R)
                yield
                kb.tt(f2(Rn), pC[:, 0:GW], f2(R), ALU.add)
                yield
                Q, QP, R = Qn, QPn, Rn
            pD, pE, pF = B[0], B[1], B[2]
            mmg(pD, R, None)
            mmg(pE, T["CN"], R)
            yield
            kb.acopy(f2(T["T0"]), pD[:, 0:GW])
            kb.copy(f2(T["Z"]), pE[:, 0:GW])
            yield
            mmg(pF, T["T0"], T["Z"])
            yield
            kb.tt(f2(T["TT"]), pF[:, 0:GW], f2(R), ALU.add)
            yield
            psU, psW = B[3], B[0]
            for m in range(G):
                tile = gi * G + m
                kb.mm(psU[:, m * 64:(m + 1) * 64], T["TT"][:, m, :], VB[:, tile, :], start=True, stop=True)
            for m in range(G):
                tile = gi * G + m
                kb.mm(psW[0:64, m * 128:(m + 1) * 128], KBG[:, tile, :], T["TT"][:, m, :], start=True, stop=True)
            yield
            kb.acopy(f2(U[dr][:, gi * G:(gi + 1) * G, :]), psU[:, 0:G * 64])
            kb.copy(WT[sb_:sb_ + 64, tsl], psW[0:64, 0:GW])

        for gi in range(NG):
            gens = [titer(0, gi), titer(1, gi)]
            alive = [True, True]
            while any(alive):
                for i in range(2):
                    if alive[i]:
                        try:
                            next(gens[i])
                        except StopIteration:
                            alive[i] = False
        if hp == 0 and hh == 0:
            self.dump("AT0", AT[0][:, :, :])
            self.dump("AT1", AT[1][:, :, :])
            self.dump("U0", U[0][:, :, :])
            self.dump("U1", U[1][:, :, :])
            self.dump("WT", WT[:, :])
            self.dump("QGT", QGT[:, :])
            self.dump("KD0", KD[0][:, :, :])
            self.dump("DECB", DECB[:, :, :])
            self.dump("TT", tb["TT"][:, :, :])
            self.dump("NN", tb["NN"][:, :, :])
        if self.cstop == "tstage":
            return
        kb.memset(SF[:, :], 0.0)
        kb.memset(SBb[:, :], 0.0)
        for s in range(32):
            for dr in range(2):
                c = s if dr == 0 else 31 - s
                m = c // 2
                pb = (c % 2) * 64
                sb_ = dr * 64
                bW, bO, bS = self.PS[dr * 4], self.PS[dr * 4 + 1], self.PS[dr * 4 + 2]
                vn = VN[dr * 2 + s % 2]
                csl = slice(c * 64, (c + 1) * 64)
                kb.mm(bW[0:64, 0:64], WT[sb_:sb_ + 64, csl], SBb[sb_:sb_ + 64, :], start=True, stop=True)
                kb.tt(vn[pb:pb + 64, :], U[dr][pb:pb + 64, m, :], bW[0:64, 0:64], ALU.subtract)
                kb.mm(bO[0:64, 0:64], QGT[sb_:sb_ + 64, csl], SBb[sb_:sb_ + 64, :], start=True, stop=False, inc=False)
                kb.mm(bO[0:64, 0:64], AT[dr][pb:pb + 64, m, pb:pb + 64], vn[pb:pb + 64, :], start=False, stop=True)
                kb.acopy(OF[dr][pb:pb + 64, m, :], bO[0:64, 0:64])
                kb.mm(bS[0:64, 0:64], KD[dr][pb:pb + 64, m, :], vn[pb:pb + 64, :], start=True, stop=True)
                kb.stt(SF[sb_:sb_ + 64, :], SF[sb_:sb_ + 64, :], DECB[sb_:sb_ + 64, dr, c:c + 1], bS[0:64, 0:64],
                       ALU.mult, ALU.add)
                kb.acopy(SBb[sb_:sb_ + 64, :], SF[sb_:sb_ + 64, :])
        if hp == 0 and hh == 0:
            self.dump("OF0", OF[0][:, :, :])
            self.dump("OF1", OF[1][:, :, :])
        if self.cstop == "scan":
            return
        OS = OF[0]
        kb.tt(OS[:, :, :], OF[0][:, :, :], OF[1][:, :, :], ALU.add)
        kb.tt(OF[1][:, :, :], OS[:, :, :], OS[:, :, :], ALU.mult)
        MS = self.aalloc([128, 16], F32)
        kb.op(kb.dve, lambda: nc.vector.tensor_reduce(out=MS[:, :], in_=OF[1][:, :, :], axis=AX.X, op=ALU.add),
              reads=[OF[1][:, :, :]], writes=[MS[:, :]])
        self.rsqrt(MS[:, :], MS[:, :], 1.0 / 64)
        kb.tt(ON[:, :, bq:bq + 64], OS[:, :, :], self.bcast_last(MS[:, :].rearrange("p (a o) -> p a o", o=1), 64), ALU.mult)


def prepare(inputs):
    cols = make_cols(inputs)
    shared = {"cols": cols.array()}
    for nm in ("ffn1", "ffn2"):
        shared[nm + "_wg"] = np.ascontiguousarray(inputs[nm + "_w_gate"], dtype=np.float32)
        shared[nm + "_wu"] = np.ascontiguousarray(inputs[nm + "_w_up"], dtype=np.float32)
        shared[nm + "_wd"] = np.ascontiguousarray(inputs[nm + "_w_down"], dtype=np.float32)
    shared["win"] = np.ascontiguousarray(np.asarray(inputs["w_in"], np.float32)[:, :, win_perm()])
    shared["wout"] = np.ascontiguousarray(inputs["w_out"], dtype=np.float32)
    shared["relb"] = np.ascontiguousarray(inputs["rel_bias"], dtype=np.float32)
    shared.update(make_consts())
    return cols, shared


def run(inputs, cores=8, stop_after=None, parts="ABC", debug=False, cstop=None, ret_all=False):
    inputs = {k: np.asarray(v) for k, v in inputs.items()}
    cols, shared = prepare(inputs)
    prog = Prog(len(cols.cols), cols.idx, stop_after=stop_after, parts=parts)
    prog.debug = debug
    prog.cstop = cstop
    nc = prog.build()
    x = inputs["x"]
    in_maps = []
    for c in range(cores):
        m = dict(shared)
        m["xT"] = np.ascontiguousarray(x[c].T)
        in_maps.append(m)
    res = run_bass_kernel_spmd(nc, in_maps, core_ids=list(range(cores)))
    out = np.stack([np.ascontiguousarray(res.results[c]["yT"].T) for c in range(cores)], axis=0)
    if ret_all:
        return out.astype(np.float32), res.results
    return out.astype(np.float32)


def kernel(**inputs):
    return run(inputs, cores=8)
```

```python
import contextlib
import math
import numpy as np
import concourse.bass as bass
import concourse.mybir as mybir
from concourse.bass_utils import run_bass_kernel_spmd

F32 = mybir.dt.float32
BF16 = mybir.dt.bfloat16
AF = mybir.ActivationFunctionType
ALU = mybir.AluOpType
AX = mybir.AxisListType

S = 2048
D = 1024
DFF = 2816
NL = 2
NF = DFF // 128
EPS = 1e-6
GROUPS = ((0, 8), (8, 16), (16, 22))


class Sem:
    def __init__(self, h, name):
        self.h = h
        self.cnt = 0
        self.name = name


class Eng:
    def __init__(self, name, h, sem, selfsync):
        self.name = name
        self.h = h
        self.sem = sem
        self.selfsync = selfsync
        self.known = {}


def _box(ap):
    t = ap.tensor
    name = t.name
    space = str(ap.space)
    if "SB" not in space and "PSUM" not in space:
        return (name, 0, 1, 0, 1)
    dims = ap.ap
    shape = t.shape
    row = 1
    for s in list(shape)[1:]:
        row *= int(s)
    off = int(ap.offset)
    p0 = off // row
    f0 = off % row
    pstep, pcnt = dims[0]
    ext = 0
    for st, cn in dims[1:]:
        ext += (cn - 1) * abs(st)
    if pstep == 0:
        pcnt = 1
    sz = mybir.dt.size(ap.dtype)
    if "PSUM" in space:
        return (name, 0, 128, 0, 2048)
    return (name, p0, p0 + pcnt, f0 * sz, (f0 + ext + 1) * sz)


BK = 2048


def _bks(b):
    return range(b[3] // BK, (b[4] - 1) // BK + 1)


class KB:
    def __init__(self, nc, es):
        self.nc = nc
        self.es = es
        self.recs = {}
        self.notrack = set()
        self.pe = Eng("pe", nc.tensor, self.newsem("s_pe"), False)
        self.act = Eng("act", nc.scalar, self.newsem("s_act"), True)
        self.dve = Eng("dve", nc.vector, self.newsem("s_dve"), True)
        self.pool = Eng("pool", nc.gpsimd, self.newsem("s_pool"), True)
        self.sp = Eng("sp", nc.sync, self.newsem("s_sp"), True)
        self.dsems = {}
        self.nwaits = 0
        self.nops = 0

    def newsem(self, name):
        return Sem(self.es.enter_context(self.nc.semaphore(name)), name)

    def dsem(self, *key):
        if key not in self.dsems:
            self.dsems[key] = self.newsem("d_" + "_".join(str(k) for k in key))
        return self.dsems[key]

    def sb(self, name, shape, dt):
        return self.es.enter_context(self.nc.sbuf_tensor(name, list(shape), dt))

    def ps(self, name, shape, dt):
        return self.es.enter_context(self.nc.psum_tensor(name, list(shape), dt))

    def _collect(self, rb, wb):
        need = {}

        def add(s, v):
            if need.get(s, 0) < v:
                need[s] = v

        for b in rb:
            for k in _bks(b):
                for r in self.recs.get((b[0], k), ()):
                    if r[5] == "w" and r[1] < b[2] and b[1] < r[2] and r[3] < b[4] and b[3] < r[4]:
                        add(r[6], r[7])
        for b in wb:
            for k in _bks(b):
                for r in self.recs.get((b[0], k), ()):
                    if r[1] < b[2] and b[1] < r[2] and r[3] < b[4] and b[3] < r[4]:
                        add(r[6], r[7])
        return need

    def _commit(self, rb, wb, sem, val):
        for b in wb:
            rec = [b[0], b[1], b[2], b[3], b[4], "w", sem, val]
            for k in _bks(b):
                lst = self.recs.setdefault((b[0], k), [])
                lst[:] = [r for r in lst if not (b[1] <= r[1] and r[2] <= b[2] and b[3] <= r[3] and r[4] <= b[4])]
                lst.append(rec)
        for b in rb:
            rec = None
            for k in _bks(b):
                lst = self.recs.setdefault((b[0], k), [])
                for r in lst:
                    if r[5] == "r" and r[6] is sem and r[1] == b[1] and r[2] == b[2] and r[3] == b[3] and r[4] == b[4]:
                        r[7] = max(r[7], val)
                        break
                else:
                    if rec is None:
                        rec = [b[0], b[1], b[2], b[3], b[4], "r", sem, val]
                    lst.append(rec)

    def _boxes(self, aps):
        out = []
        for a in aps:
            b = _box(a)
            if b[0] in self.notrack:
                continue
            out.append(b)
        return out

    def _waits(self, eng, need):
        for s, v in need.items():
            if v <= 0:
                continue
            if s is eng.sem and not eng.selfsync:
                continue
            if eng.known.get(s, 0) >= v:
                continue
            eng.h.wait_ge(s.h, v)
            eng.known[s] = v
            self.nwaits += 1

    def op(self, eng, fn, reads=(), writes=(), inc=True):
        rb = self._boxes(reads)
        wb = self._boxes(writes)
        if eng is not self.pe:
            wb = wb + [b for b in rb if b[0].startswith("ps")]
            rb = [b for b in rb if not b[0].startswith("ps")]
        need = self._collect(rb, wb)
        self._waits(eng, need)
        ins = fn()
        self.nops += 1
        if inc:
            eng.sem.cnt += 1
            ins.then_inc(eng.sem.h, 1)
            val = eng.sem.cnt
        else:
            val = eng.sem.cnt + 1
        self._commit(rb, wb, eng.sem, val)
        return ins

    def dma(self, q, out, in_, dsem):
        rb = self._boxes([in_])
        wb = self._boxes([out])
        need = self._collect(rb, wb)
        if dsem.cnt > 0:
            need[dsem] = max(need.get(dsem, 0), dsem.cnt)
        self._waits(q, need)
        ins = q.h.dma_start(out=out, in_=in_)
        ins.then_inc(dsem.h, 16)
        dsem.cnt += 16
        self.nops += 1
        self._commit(rb, wb, dsem, dsem.cnt)

    def mm(self, out, lhsT, rhs, start, stop, inc=None):
        if inc is None:
            inc = stop
        return self.op(self.pe, lambda: self.nc.tensor.matmul(out, lhsT=lhsT, rhs=rhs, start=start, stop=stop),
                       reads=[lhsT, rhs], writes=[out], inc=inc)

    def activation(self, out, in_, func, bias=None, scale=None, extra_reads=()):
        kw = {}
        rd = [in_] + list(extra_reads)
        if bias is not None:
            kw["bias"] = bias
            if not isinstance(bias, (int, float)):
                rd.append(bias)
        if scale is not None:
            kw["scale"] = scale
            if not isinstance(scale, (int, float)):
                rd.append(scale)
        return self.op(self.act, lambda: self.nc.scalar.activation(out=out, in_=in_, func=func, **kw),
                       reads=rd, writes=[out])

    def tt(self, out, in0, in1, op, eng=None):
        eng = eng or self.dve
        return self.op(eng, lambda: eng.h.tensor_tensor(out=out, in0=in0, in1=in1, op=op),
                       reads=[in0, in1], writes=[out])

    def ts(self, out, in0, s1, s2, op0, op1=None, eng=None):
        eng = eng or self.dve
        rd = [in0]
        if not isinstance(s1, (int, float)):
            rd.append(s1)
        if s2 is not None and not isinstance(s2, (int, float)):
            rd.append(s2)
        if op1 is None:
            return self.op(eng, lambda: eng.h.tensor_scalar(out=out, in0=in0, scalar1=s1, scalar2=None, op0=op0),
                           reads=rd, writes=[out])
        return self.op(eng, lambda: eng.h.tensor_scalar(out=out, in0=in0, scalar1=s1, scalar2=s2, op0=op0, op1=op1),
                       reads=rd, writes=[out])

    def stt(self, out, in0, scalar, in1, op0, op1, eng=None):
        eng = eng or self.dve
        rd = [in0, in1]
        if not isinstance(scalar, (int, float)):
            rd.append(scalar)
        return self.op(eng, lambda: eng.h.scalar_tensor_tensor(out=out, in0=in0, scalar=scalar, in1=in1, op0=op0, op1=op1),
                       reads=rd, writes=[out])

    def copy(self, out, in_, eng=None):
        eng = eng or self.dve
        return self.op(eng, lambda: eng.h.tensor_copy(out=out, in_=in_), reads=[in_], writes=[out])

    def acopy(self, out, in_):
        return self.op(self.act, lambda: self.nc.scalar.copy(out=out, in_=in_), reads=[in_], writes=[out])

    def memset(self, ap, val, eng=None):
        eng = eng or self.dve
        return self.op(eng, lambda: eng.h.memset(ap, val), reads=[], writes=[ap])


class Cols:
    def __init__(self):
        self.cols = []
        self.idx = {}

    def add(self, key, vec):
        v = np.zeros(128, np.float32)
        vec = np.asarray(vec, np.float32).reshape(-1)
        v[: vec.shape[0]] = vec
        self.idx[key] = len(self.cols)
        self.cols.append(v)

    def add_chunks(self, key, vec):
        vec = np.asarray(vec, np.float32).reshape(-1, 128)
        for i in range(vec.shape[0]):
            self.add((key, i), vec[i])

    def array(self):
        return np.ascontiguousarray(np.stack(self.cols, axis=1))


def _swap64(v):
    v = np.asarray(v)
    return np.concatenate([v[32:64], v[0:32]])


def make_cols(inp):
    c = Cols()
    for l in range(NL):
        c.add_chunks(("ffn1_norm", l), inp["ffn1_norm"][l])
        c.add_chunks(("mix_norm", l), inp["mix_norm"][l])
        c.add_chunks(("ffn2_norm", l), inp["ffn2_norm"][l])
        for nm in ("a_q_norm", "a_k_norm", "b_q_norm", "b_k_norm", "c_out_norm"):
            g = inp[nm][l]
            c.add((nm, l), np.concatenate([g, g]))
            c.add((nm + "_sw", l), np.concatenate([_swap64(g), _swap64(g)]))
        cw = inp["c_conv"][l]
        for ch in range(9):
            for j in range(5):
                c.add(("conv", l, ch, j), cw[j, ch * 128:(ch + 1) * 128])
        c.add(("alog", l), inp["c_A_log"][l].reshape(-1))
        c.add(("dtb", l), inp["c_dt_bias"][l].reshape(-1))
    c.add("mF", [1.0] * 6 + [0.0] * 6)
    c.add("mB", [0.0] * 6 + [1.0] * 6)
    c.add("one", np.ones(128))
    c.add("eps", np.full(128, EPS))
    return c


NWC = 28 * 128 + 24


def win_perm():
    aq, ak, av, bq, bk, bv, cq, ck, cv, cz, cb, ca = 0, 256, 384, 512, 896, 1280, 1664, 2048, 2432, 2816, 3200, 3212

    def head(base, h):
        return list(range(base + h * 64, base + h * 64 + 64))

    def swp(cols):
        return cols[32:64] + cols[0:32]

    p = []
    qa0 = head(aq, 0) + head(aq, 2)
    qa1 = head(aq, 1) + head(aq, 3)
    p += qa0 + qa1
    p += swp(head(aq, 0)) + swp(head(aq, 2)) + swp(head(aq, 1)) + swp(head(aq, 3))
    ka = head(ak, 0) + head(ak, 1)
    p += ka + swp(head(ak, 0)) + swp(head(ak, 1))
    p += list(range(bq, bq + 384)) + list(range(bk, bk + 384))
    p += list(range(cq, cq + 384)) + list(range(ck, ck + 384)) + list(range(cv, cv + 384)) + list(range(cz, cz + 384))
    p += list(range(av, av + 128)) + list(range(bv, bv + 384))
    p += list(range(cb, cb + 12)) + list(range(ca, ca + 12))
    assert len(p) == NWC
    return np.array(p)


LW = 3072
R0 = 1535
TSW = 2944


def t5_bucket_np(rel):
    rel = np.asarray(rel, np.int64)
    half, exact = 16, 8
    sign = np.where(rel > 0, half, 0)
    n = np.abs(rel)
    nf = np.maximum(n, 1).astype(np.float32)
    large = exact + (np.log(nf / np.float32(exact)) / np.float32(math.log(1024 / exact)) * np.float32(half - exact)).astype(np.int32)
    large = np.minimum(large, half - 1)
    return sign + np.where(n < exact, n, large)


def make_consts():
    c = {}
    I = np.eye(128, dtype=np.float32)
    ii = np.arange(128)
    same64 = (ii[:, None] // 64) == (ii[None, :] // 64)
    same32 = (ii[:, None] // 32) == (ii[None, :] // 32)
    row = ii[:, None]
    col = ii[None, :]
    NEG = -30000.0
    blocks = [
        I,
        same64.astype(np.float32),
        I[::-1].copy(),
        same32.astype(np.float32),
        np.where(same64 & (col < row), 0.0, NEG),
        np.where(same64 & (col > row), 0.0, NEG),
        np.where(same64 & (row <= col), 0.0, NEG),
        np.where(same64 & (row >= col), 0.0, NEG),
        (~same32).astype(np.float32),
    ]
    c["cm"] = np.ascontiguousarray(np.concatenate([b.astype(np.float32) for b in blocks], axis=1))
    sel = np.zeros((12, 12, 128), np.float32)
    for r in range(12):
        sel[r, r, :] = 1.0
    c["sel"] = sel.reshape(12, 12 * 128)
    t = np.arange(S)
    rw = (t // 64).astype(np.float32)
    cl = (t % 64).astype(np.float32)
    inv = (10000.0 ** (-np.arange(16, dtype=np.float32) / 16)).astype(np.float32)
    ang = np.concatenate([rw[:, None] * inv, cl[:, None] * inv], axis=-1).astype(np.float32)
    cos = np.cos(ang).astype(np.float32).T
    sin = np.sin(ang).astype(np.float32).T
    COS = np.concatenate([cos, cos, cos, cos], axis=0)
    SIN = np.concatenate([-sin, sin, -sin, sin], axis=0)
    c["rope"] = np.ascontiguousarray(np.stack([COS, SIN], axis=0).astype(np.float32))
    i = np.arange(LW)
    r = R0 - i
    bkt = t5_bucket_np(r)
    oh = np.zeros((32, LW), np.float32)
    oh[bkt, i] = 1.0
    mult = ((np.abs(r) <= 64).astype(np.float32) + ((r % 4 == 0) & (np.abs(r) <= 256)).astype(np.float32)
            + ((r % 16 == 0) & (np.abs(r) <= 1024)).astype(np.float32))
    c["ohrev"] = oh
    c["multrev"] = np.ascontiguousarray(np.tile(mult[None, :], (6, 1)).astype(np.float32))
    return c


ARN = 50048


class Prog:
    def __init__(self, ncols, colidx, stop_after=None, parts="ABC"):
        self.colidx = colidx
        self.stop_after = stop_after
        self.parts = parts
        self.dbg_sems = []
        self.debug = False
        self.cstop = None
        nc = bass.Bass("TRN2", target_bir_lowering=False)
        self.nc = nc
        self.es = contextlib.ExitStack()
        d = {}

        def inp(name, shape, dt=F32):
            d[name] = nc.dram_tensor(name, list(shape), dt, kind="ExternalInput").ap()

        inp("xT", [D, S])
        inp("cols", [128, ncols])
        for nm in ("ffn1", "ffn2"):
            inp(nm + "_wg", [NL, D, DFF])
            inp(nm + "_wu", [NL, D, DFF])
            inp(nm + "_wd", [NL, DFF, D])
        inp("win", [NL, D, NWC])
        inp("wout", [NL, D, D])
        inp("relb", [32, 6])
        inp("cm", [128, 9 * 128])
        inp("sel", [12, 12 * 128])
        inp("rope", [2, 128, S])
        inp("ohrev", [32, LW])
        inp("multrev", [6, LW])
        self.inputs = dict(d)
        d["yT"] = nc.dram_tensor("yT", [D, S], F32, kind="ExternalOutput").ap()
        d["wrev"] = nc.dram_tensor("wrev", [6, LW], BF16, kind="Internal").ap()
        d["strips"] = nc.dram_tensor("strips", [6, 128, TSW], BF16, kind="Internal").ap()
        self.d = d
        self.ncols = ncols

    def col(self, key, n=128):
        i = self.colidx[key]
        return self.COLS[0:n, i:i + 1]

    def build(self):
        nc = self.nc
        with self.es:
            kb = KB(nc, self.es)
            self.kb = kb
            for k in self.inputs:
                kb.notrack.add(self.inputs[k].tensor.name)
            self.alloc()
            self.body()
        return nc

    def areset(self, base=0):
        self.aptr = base

    def aalloc(self, shape, dt):
        n = 1
        for x in shape[1:]:
            n *= x
        ne = n * (2 if dt == F32 else 1)
        off = self.aptr
        off += off % 2
        assert off + ne <= ARN, ("arena overflow", off, ne)
        self.aptr = off + ne
        v = self.AR[:, off:off + ne]
        if dt == F32:
            v = v.bitcast(F32)
        v = v[0:shape[0]]
        if len(shape) == 3:
            v = v.rearrange("p (a b) -> p a b", a=shape[1])
        return v

    def alloc(self):
        kb = self.kb
        self.H = kb.sb("H", [128, 8, S], F32)
        self.HN = kb.sb("HN", [128, 8, S], BF16)
        self.COLS = kb.sb("COLS", [128, self.ncols], F32)
        self.CMF = kb.sb("CMF", [128, 9, 128], F32)
        self.CMB = kb.sb("CMB", [128, 9, 128], BF16)
        self.SELF = kb.sb("SELF", [12, 12, 128], F32)
        self.ONES = kb.sb("ONES", [128, 128], BF16)
        self.AR = kb.sb("AR", [128, ARN], BF16)
        self.PS = [kb.ps("ps%d" % i, [128, 512], F32) for i in range(8)]

    def ffn_alloc(self):
        self.areset()
        self.WG = [self.aalloc([128, 8, 256], BF16) for i in range(2)]
        self.WU = [self.aalloc([128, 8, 256], BF16) for i in range(2)]
        self.WD = [self.aalloc([128, D], BF16) for i in range(8)]
        self.HID = self.aalloc([128, 8, S], BF16)
        self.SG = [self.aalloc([128, 512], F32) for i in range(4)]
        self.SQ = self.aalloc([128, 8, 512], BF16)
        self.RSTD = self.aalloc([128, 512], F32)

    def done(self, tag):
        return self.stop_after == tag

    def dump(self, name, ap):
        if not getattr(self, "debug", False):
            return
        t = self.nc.dram_tensor("dbg_" + name, list(ap.shape), F32, kind="ExternalOutput").ap()
        sm = self.kb.dsem("dbg", name)
        self.kb.dma(self.kb.pool, t, ap, sm)
        self.dbg_sems.append(sm)

    def body(self):
        kb = self.kb
        nc = self.nc
        d = self.d
        kb.dma(kb.sp, self.COLS[:, :], d["cols"], kb.dsem("cols"))
        kb.dma(kb.sp, self.CMF[:, :, :], d["cm"].rearrange("p (a b) -> p a b", a=9), kb.dsem("cm"))
        kb.dma(kb.sp, self.SELF[:, :, :], d["sel"].rearrange("p (a b) -> p a b", a=12), kb.dsem("sel"))
        xv = d["xT"].rearrange("(dc p) t -> p dc t", p=128)
        for dc in range(8):
            kb.dma(kb.sp, self.H[:, dc, :], xv[:, dc, :], kb.dsem("x", dc % 4))
        kb.memset(self.ONES[:, :], 1.0)
        kb.copy(self.CMB[:, :, :], self.CMF[:, :, :])
        self.IDB = self.CMB[:, 0, :]
        self.BD64B = self.CMB[:, 1, :]
        if "B" in self.parts:
            self.build_strips()
        for l in range(NL):
            self.ffn_alloc()
            if not getattr(self, "skip_ffn", False):
                self.rmsnorm(("ffn1_norm", l))
                self.ffn(d["ffn1_wg"][l], d["ffn1_wu"][l], d["ffn1_wd"][l])
            if self.done(("ffn1", l)):
                break
            self.rmsnorm(("mix_norm", l))
            self.mixer(l)
            if self.done(("mix", l)):
                break
            self.ffn_alloc()
            self.rmsnorm(("ffn2_norm", l))
            self.ffn(d["ffn2_wg"][l], d["ffn2_wu"][l], d["ffn2_wd"][l])
            if self.done(("ffn2", l)):
                break
        yv = d["yT"].rearrange("(dc p) t -> p dc t", p=128)
        osems = []
        for dc in range(8):
            sm = kb.dsem("y", dc)
            kb.dma(kb.sp, yv[:, dc, :], self.H[:, dc, :], sm)
            osems.append(sm)
        for sm in osems + self.dbg_sems:
            nc.sync.wait_ge(sm.h, sm.cnt)

    def rsqrt(self, out, in_, scale):
        kb = self.kb
        n = out.shape[0]
        kb.activation(out, in_, AF.Sqrt, bias=self.col("eps", n), scale=scale)
        kb.op(kb.dve, lambda: self.nc.vector.reciprocal(out=out, in_=out), reads=[out], writes=[out])

    def rmsnorm(self, gkey):
        kb = self.kb
        for tc in range(4):
            tsl = slice(tc * 512, (tc + 1) * 512)
            for dc in range(8):
                kb.activation(self.SQ[:, dc, :], self.H[:, dc, tsl], AF.Square)
            bank = self.PS[tc % 2]
            for dc in range(8):
                kb.mm(bank[:, :], self.ONES[:, :], self.SQ[:, dc, :], start=(dc == 0), stop=(dc == 7))
            self.rsqrt(self.RSTD[:, :], bank[:, :], 1.0 / D)
            for dc in range(8):
                kb.stt(self.HN[:, dc, tsl], self.H[:, dc, tsl], self.col((gkey, dc)), self.RSTD[:, :],
                       ALU.mult, ALU.mult)

    def ffn(self, wg, wu, wd):
        kb = self.kb
        wgv = wg.rearrange("(kc p) f -> p kc f", p=128)
        wuv = wu.rearrange("(kc p) f -> p kc f", p=128)
        wdv = wd.rearrange("(f p) d -> p f d", p=128)

        def load_pair(p):
            s = p % 2
            kb.dma(kb.pool, self.WG[s][:, :, :], wgv[:, :, p * 256:(p + 1) * 256], kb.dsem("wg", s))
            kb.dma(kb.pool, self.WU[s][:, :, :], wuv[:, :, p * 256:(p + 1) * 256], kb.dsem("wu", s))

        load_pair(0)
        load_pair(1)
        for (f0, f1) in GROUPS:
            for i, f in enumerate(range(f0, f1)):
                kb.dma(kb.pool, self.WD[i][:, :], wdv[:, f, :], kb.dsem("wd", i))
            for f in range(f0, f1):
                p = f // 2
                s = p % 2
                csl = slice((f % 2) * 128, (f % 2) * 128 + 128)
                for (W, b0) in ((self.WG[s], 0), (self.WU[s], 4)):
                    for kc in range(8):
                        for tc in range(4):
                            kb.mm(self.PS[b0 + tc][:, :], W[:, kc, csl], self.HN[:, kc, tc * 512:(tc + 1) * 512],
                                  start=(kc == 0), stop=(kc == 7))
                if f % 2 == 1 and p + 2 < NF // 2:
                    load_pair(p + 2)
                for tc in range(4):
                    kb.activation(self.SG[tc][:, :], self.PS[tc][:, :], AF.Silu)
                    kb.tt(self.HID[:, f - f0, tc * 512:(tc + 1) * 512], self.SG[tc][:, :], self.PS[4 + tc][:, :],
                          ALU.mult)
            nfl = f1 - f0
            for dc in range(8):
                b0 = (dc % 2) * 4
                for fl in range(nfl):
                    for tc in range(4):
                        kb.mm(self.PS[b0 + tc][:, :], self.WD[fl][:, dc * 128:(dc + 1) * 128],
                              self.HID[:, fl, tc * 512:(tc + 1) * 512], start=(fl == 0), stop=(fl == nfl - 1))
                for tc in range(4):
                    hs = self.H[:, dc, tc * 512:(tc + 1) * 512]
                    kb.stt(hs, self.PS[b0 + tc][:, :], 0.5, hs, ALU.mult, ALU.add)

    def load_win(self, l, dst, c0, ncol, key):
        src = self.d["win"][l].rearrange("(kc p) c -> p kc c", p=128)[:, :, c0:c0 + ncol]
        self.kb.dma(self.kb.pool, dst, src, self.kb.dsem("win", key))

    def proj(self, bank, W, c0, tc, ncol=128):
        kb = self.kb
        for kc in range(8):
            kb.mm(bank[0:ncol, :], W[:, kc, c0:c0 + ncol], self.HN[:, kc, tc * 512:(tc + 1) * 512],
                  start=(kc == 0), stop=(kc == 7))

    def wout(self, l, mcs, srcs):
        kb = self.kb
        wo = []
        for i, mc in enumerate(mcs):
            w = self.aalloc([128, D], BF16)
            kb.dma(kb.pool, w[:, :], self.d["wout"][l][mc * 128:(mc + 1) * 128, :], kb.dsem("wo", i))
            wo.append(w)
        n = len(mcs)
        for dc in range(8):
            b0 = (dc % 2) * 4
            for i in range(n):
                for tc in range(4):
                    kb.mm(self.PS[b0 + tc][:, :], wo[i][:, dc * 128:(dc + 1) * 128],
                          srcs[i][:, tc * 512:(tc + 1) * 512], start=(i == 0), stop=(i == n - 1))
            for tc in range(4):
                hs = self.H[:, dc, tc * 512:(tc + 1) * 512]
                kb.tt(hs, self.PS[b0 + tc][:, :], hs, ALU.add)

    def seg2(self, base_ap, delta):
        a = base_ap.ap
        return bass.AP(base_ap.tensor, base_ap.offset, [[a[0][0], a[0][1]], [delta, 2], [1, 64]])

    def mixer(self, l):
        self.areset()
        if "A" in self.parts or "B" in self.parts:
            self.vtok(l)
        base = self.aptr
        if "A" in self.parts:
            self.areset(base)
            self.mixer_a(l)
        if "B" in self.parts:
            self.areset(base)
            self.mixer_b(l)
        if "C" in self.parts:
            self.areset()
            self.mixer_c(l)

    def vtok(self, l):
        kb = self.kb
        self.VT = self.aalloc([128, 16, 1024], BF16)
        mark = self.aptr
        WV = self.aalloc([128, 8, 512], BF16)
        self.load_win(l, WV[:, :, :], 24 * 128, 512, "wv")
        kb.memset(self.VT.rearrange("p m (s c) -> p (m s) c", s=8)[:, :, 64:128], 1.0)
        for m in range(16):
            bank = self.PS[m % 2]
            for kc in range(8):
                kb.mm(bank[:, :], self.HN[:, kc, m * 128:(m + 1) * 128], WV[:, kc, :], start=(kc == 0), stop=(kc == 7))
            vdst = self.VT[:, m, :].rearrange("p (s c) -> p s c", s=8)[:, :, 0:64]
            vsrc = bank[:, :].rearrange("p (s c) -> p s c", s=8)
            if m % 2 == 0:
                kb.copy(vdst, vsrc)
            else:
                kb.acopy(vdst, vsrc)
        self.aptr = mark + 0

    def attention(self, qsrc, ksrc, vcol, dst, strip=None):
        kb = self.kb
        for qc in range(4):
            qsl = slice(qc * 512, (qc + 1) * 512)
            psO = self.PS[4 + qc % 2]
            kts = []
            for kt in range(16):
                dk = kt * 128 - qc * 512
                if strip is not None and not (-1024 <= dk <= 1408):
                    continue
                kts.append(kt)
            NB = 4
            LA = 3

            def emit_qk(n):
                kt = kts[n]
                psS = self.PS[n % NB]
                pt = self.PT[n % NB]
                kb.mm(psS[:, :], ksrc[:, kt * 128:(kt + 1) * 128], qsrc[:, qsl], start=True, stop=True)
                kb.activation(pt[:, :], psS[:, :], AF.Exp, scale=0.125)
                if strip is not None:
                    cs = 1408 - (kt * 128 - qc * 512)
                    kb.tt(pt[:, :], pt[:, :], strip[:, cs:cs + 512], ALU.mult)

            def emit_pv(n):
                kt = kts[n]
                vl = self.VT[:, kt, vcol * 2:vcol * 2 + 128]
                kb.mm(psO[:, :], vl, self.PT[n % NB][:, :], start=(n == 0), stop=(n == len(kts) - 1))

            for n in range(len(kts) + LA):
                if n < len(kts):
                    emit_qk(n)
                if n >= LA:
                    emit_pv(n - LA)
            kb.op(kb.dve, lambda: self.nc.vector.reciprocal(out=self.REC[0:64, :], in_=psO[64:128, :]),
                  reads=[psO[64:128, :]], writes=[self.REC[0:64, :]])
            kb.tt(dst[:, qsl], psO[0:64, :], self.REC[0:64, :], ALU.mult)

    def qknorm_chunk(self, l, W, c0, cs0, gkey, dst, tc, rope, split=None):
        kb = self.kb
        tsl = slice(tc * 512, (tc + 1) * 512)
        px = self.PS[0]
        self.proj(px, W, c0, tc)
        kb.activation(self.SQT[:, :], px[:, :], AF.Square)
        pst = self.PS[2]
        kb.mm(pst[:, :], self.BD64B, self.SQT[:, :], start=True, stop=True)
        self.rsqrt(self.RST[:, :], pst[:, :], 1.0 / 64)
        if not rope:
            if split is None:
                kb.stt(dst[:, tsl], px[:, :], self.col((gkey, l)), self.RST[:, :], ALU.mult, ALU.mult)
            else:
                for hf in range(2):
                    ps_ = slice(hf * 64, hf * 64 + 64)
                    kb.stt(split[hf][ps_, tsl], px[ps_, :], self.col((gkey, l))[ps_, :], self.RST[ps_, :],
                           ALU.mult, ALU.mult)
            return
        pw = self.PS[1]
        self.proj(pw, W, cs0, tc)
        kb.stt(self.T0[:, :], px[:, :], self.col((gkey, l)), self.RST[:, :], ALU.mult, ALU.mult)
        kb.stt(self.T1[:, :], pw[:, :], self.col((gkey + "_sw", l)), self.RST[:, :], ALU.mult, ALU.mult)
        kb.tt(self.T0[:, :], self.T0[:, :], self.CS[:, 0, :], ALU.mult)
        kb.tt(self.T1[:, :], self.T1[:, :], self.CS[:, 1, :], ALU.mult)
        if split is None:
            kb.tt(dst[:, tsl], self.T0[:, :], self.T1[:, :], ALU.add)
        else:
            for hf in range(2):
                ps_ = slice(hf * 64, hf * 64 + 64)
                kb.tt(split[hf][ps_, tsl], self.T0[ps_, :], self.T1[ps_, :], ALU.add)

    def att_tmps(self):
        self.PT = [self.aalloc([128, 512], BF16) for i in range(4)]
        self.REC = self.aalloc([128, 512], F32)
        self.SQT = self.aalloc([128, 512], BF16)
        self.RST = self.aalloc([128, 512], F32)
        self.T0 = self.aalloc([128, 512], F32)
        self.T1 = self.aalloc([128, 512], F32)

    def mixer_a(self, l):
        kb = self.kb
        self.att_tmps()
        self.CS = self.aalloc([128, 2, 512], F32)
        WA = self.aalloc([128, 8, 768], BF16)
        self.load_win(l, WA[:, :, :], 0, 768, "wa")
        QZ = [[self.aalloc([128, S], BF16) for j in range(2)] for i in range(2)]
        KT = self.aalloc([128, S], BF16)
        OUT = [self.aalloc([128, S], BF16) for i in range(2)]
        for i in range(2):
            kb.memset(QZ[i][0][64:128, :], 0.0)
            kb.memset(QZ[i][1][0:64, :], 0.0)
        rv = self.d["rope"].rearrange("c p t -> p c t")
        for tc in range(4):
            kb.dma(kb.sp, self.CS[:, :, :], rv[:, :, tc * 512:(tc + 1) * 512], kb.dsem("rope"))
            self.qknorm_chunk(l, WA, 0, 256, "a_q_norm", None, tc, True, split=QZ[0])
            self.qknorm_chunk(l, WA, 128, 384, "a_q_norm", None, tc, True, split=QZ[1])
            self.qknorm_chunk(l, WA, 512, 640, "a_k_norm", KT, tc, True)
        for h in range(4):
            g = h // 2
            self.attention(QZ[h % 2][g], KT, g * 64, OUT[h // 2][(h % 2) * 64:(h % 2) * 64 + 64, :])
        self.wout(l, [0, 1], OUT)

    def build_strips(self):
        kb = self.kb
        self.areset()
        d = self.d
        OH = self.aalloc([32, LW], F32)
        MU = self.aalloc([6, LW], F32)
        RB = self.aalloc([32, 6], F32)
        W6 = self.aalloc([6, LW], F32)
        W6B = self.aalloc([6, LW], BF16)
        kb.dma(kb.sp, OH[:, :], d["ohrev"], kb.dsem("oh"))
        kb.dma(kb.sp, MU[:, :], d["multrev"], kb.dsem("mu"))
        kb.dma(kb.sp, RB[:, :], d["relb"], kb.dsem("rb"))
        for n in range(LW // 512):
            sl = slice(n * 512, (n + 1) * 512)
            bank = self.PS[n % 2]
            kb.mm(bank[0:6, :], RB[:, :], OH[:, sl], start=True, stop=True)
            kb.activation(W6[:, sl], bank[0:6, :], AF.Exp)
        kb.tt(W6B[:, :], W6[:, :], MU[:, :], ALU.mult)
        kb.dma(kb.sp, d["wrev"], W6B[:, :], kb.dsem("wrev"))
        HK = self.aalloc([128, TSW], BF16)
        TSB = self.aalloc([128, TSW], BF16)
        JB = self.CMB[:, 2, :]
        for h in range(6):
            src = bass.AP(d["wrev"].tensor, h * LW, [[1, 128], [1, TSW]])
            kb.dma(kb.sp, HK[:, :], src, kb.dsem("hk"))
            for n in range((TSW + 511) // 512):
                w = min(512, TSW - n * 512)
                sl = slice(n * 512, n * 512 + w)
                bank = self.PS[n % 2]
                kb.mm(bank[:, 0:w], JB, HK[:, sl], start=True, stop=True)
                if n % 2 == 0:
                    kb.copy(TSB[:, sl], bank[:, 0:w])
                else:
                    kb.acopy(TSB[:, sl], bank[:, 0:w])
            kb.dma(kb.sp, d["strips"][h], TSB[:, :], kb.dsem("strips"))

    def mixer_b(self, l):
        kb = self.kb
        self.att_tmps()
        TS = [self.aalloc([128, TSW], BF16) for i in range(2)]
        WB = [self.aalloc([128, 8, 256], BF16) for i in range(2)]
        QZ = [[self.aalloc([128, S], BF16) for j in range(2)] for i in range(2)]
        KBf = [self.aalloc([128, S], BF16) for i in range(2)]
        OUT1 = self.aalloc([128, S], BF16)
        OUT = [OUT1, OUT1]
        for i in range(2):
            kb.memset(QZ[i][0][64:128, :], 0.0)
            kb.memset(QZ[i][1][0:64, :], 0.0)
        mark = self.aptr
        def loadw(hp):
            s = hp % 2
            self.load_win(l, WB[s][:, :, 0:128], (6 + hp) * 128, 128, ("wbq", s))
            self.load_win(l, WB[s][:, :, 128:256], (9 + hp) * 128, 128, ("wbk", s))

        loadw(0)
        for hp in range(3):
            s = hp % 2
            if hp + 1 < 3:
                loadw(hp + 1)
            for tc in range(4):
                self.qknorm_chunk(l, WB[s], 0, None, "b_q_norm", None, tc, False, split=QZ[s])
                self.qknorm_chunk(l, WB[s], 128, None, "b_k_norm", KBf[s], tc, False)
            for hh in range(2):
                h = 2 * hp + hh
                kb.dma(kb.sp, TS[hh][:, :], self.d["strips"][h], kb.dsem("ts", hh))
                self.attention(QZ[s][hh], KBf[s], 128 + h * 64,
                               OUT[s][hh * 64:(hh + 1) * 64, :], strip=TS[hh])
            self.aptr = mark
            self.wout(l, [2 + hp], [OUT[s]])

    def bcast_mid(self, ap2d, n):
        a = ap2d.ap
        return bass.AP(ap2d.tensor, ap2d.offset, [[a[0][0], a[0][1]], [0, n], [a[1][0], a[1][1]]])

    def bcast_last(self, ap, n):
        a = [list(x) for x in ap.ap]
        a[-1] = [0, n]
        return bass.AP(ap.tensor, ap.offset, a)

    def c_gates(self, l):
        kb = self.kb
        nc = self.nc
        GW = self.aalloc([128, 8, 24], BF16)
        self.load_win(l, GW[:, :, :], 28 * 128, 24, "wg12")
        self.GC = self.aalloc([12, S], F32)
        self.EG = self.aalloc([12, S], F32)
        self.TOKT = self.aalloc([128, 5, 16 * 12], F32)
        self.TOT = self.aalloc([12, 32], F32)
        self.DEC = self.aalloc([12, 32], F32)
        NEGA = self.aalloc([12, 2], F32)
        mark = self.aptr
        B0 = self.aalloc([12, S], F32)
        B1 = self.aalloc([12, S], F32)
        B2 = self.aalloc([12, S], F32)
        B3 = self.aalloc([12, S], F32)
        B4 = self.aalloc([12, S], F32)
        kb.activation(NEGA[:, 0:1], self.col(("alog", l), 12), AF.Exp)
        kb.ts(NEGA[:, 1:2], NEGA[:, 0:1], -1.0, 0.0, ALU.mult, ALU.add)
        for tc in range(4):
            tsl = slice(tc * 512, (tc + 1) * 512)
            pb_, pa_ = self.PS[0], self.PS[1]
            self.proj(pb_, GW, 0, tc, ncol=12)
            self.proj(pa_, GW, 12, tc, ncol=12)
            kb.activation(B0[:, tsl], pb_[0:12, :], AF.Sigmoid)
            kb.activation(B4[:, tsl], pa_[0:12, :], AF.Exp, bias=self.col(("dtb", l), 12))
            kb.activation(B4[:, tsl], B4[:, tsl], AF.Ln, bias=self.col("one", 12))
            kb.ts(B1[:, tsl], B4[:, tsl], NEGA[:, 1:2], 0.0, ALU.mult, ALU.add)

        if self.cstop == "g2":
            return

        def v3(b):
            return b.rearrange("p (c t) -> p c t", c=32)

        src = B1
        seq = [B2, B3, B2, B3, B2, B3]
        for k, sh in enumerate((1, 2, 4, 8, 16, 32)):
            dst = seq[k]
            kb.tt(v3(dst)[:, :, sh:64], v3(src)[:, :, sh:64], v3(src)[:, :, 0:64 - sh], ALU.add)
            kb.copy(v3(dst)[:, :, 0:sh], v3(src)[:, :, 0:sh])
            src = dst
        P = B3
        if self.cstop == "g3":
            return
        kb.copy(self.TOT[:, :], v3(P)[:, :, 63])
        totb = self.bcast_last(self.TOT[:, :].rearrange("p (c o) -> p c o", o=1), 64)
        kb.tt(v3(B2), totb, v3(P), ALU.subtract)
        kb.tt(B2[:, :], B2[:, :], B1[:, :], ALU.add)
        kb.ts(B1[:, :], P[:, :], self.col("mF", 12), 0.0, ALU.mult, ALU.add)
        kb.stt(self.GC[:, :], B2[:, :], self.col("mB", 12), B1[:, :], ALU.mult, ALU.add)
        kb.activation(self.DEC[:, :], self.TOT[:, :], AF.Exp)
        kb.tt(v3(B1), totb, v3(self.GC), ALU.subtract)
        kb.activation(B1[:, :], B1[:, :], AF.Exp)
        kb.activation(B2[:, :], B0[:, :], AF.Ln)
        kb.tt(B2[:, :], B2[:, :], self.GC[:, :], ALU.add)
        kb.ts(B3[:, :], self.GC[:, :], -1.0, 0.0, ALU.mult, ALU.add)
        kb.activation(self.EG[:, :], self.GC[:, :], AF.Exp)
        kb.tt(B4[:, :], self.EG[:, :], B0[:, :], ALU.mult)
        if self.cstop == "g4":
            return
        IDF = self.CMF[0:12, 0, 0:12]
        for q, X in enumerate((B2, B3, B4, B0, B1)):
            bank = self.PS[2 + q % 2]
            for m in range(16):
                kb.mm(bank[:, m * 12:(m + 1) * 12], X[0:12, m * 128:(m + 1) * 128], IDF, start=True, stop=True)
            kb.copy(self.TOKT[:, q, :], bank[:, 0:192])
        self.aptr = mark

    def tok(self, q, tile, r):
        return self.TOKT[:, q, tile * 12 + r:tile * 12 + r + 1]

    def tokb(self, q, r):
        base = self.TOKT[:, q, r:r + 1]
        a = base.ap
        return bass.AP(base.tensor, base.offset, [[a[0][0], a[0][1]], [12, 16], [0, 64]])

    def mixer_c(self, l):
        kb = self.kb
        self.c_gates(l)
        if self.cstop in ("g2", "g3", "g4"):
            return
        self.dump("GC", self.GC[:, :])
        self.dump("TOKT", self.TOKT[:, :, :])
        self.dump("DEC", self.DEC[:, :])
        if self.cstop == "gates":
            return
        base_pair = self.aptr
        for hp in range(3):
            self.areset(base_pair)
            self.c_pair(l, hp)

    def c_pair(self, l, hp):
        kb = self.kb
        nc = self.nc
        QT = self.aalloc([128, S], BF16)
        KT = self.aalloc([128, S], BF16)
        KTOK = self.aalloc([128, 16, 128], BF16)
        VTOK = self.aalloc([128, 16, 128], BF16)
        ON = self.aalloc([128, 16, 128], BF16)
        self.SQT = self.aalloc([128, 512], BF16)
        self.RST = self.aalloc([128, 512], F32)
        mark = self.aptr
        WC = self.aalloc([128, 8, 384], BF16)
        XS = self.aalloc([128, S + 4], F32)
        ACC = self.aalloc([128, S], F32)
        VTf = self.aalloc([128, S], BF16)
        for ci in range(3):
            self.load_win(l, WC[:, :, ci * 128:(ci + 1) * 128], (12 + 3 * ci + hp) * 128, 128, ("wc", ci))
        kb.memset(XS[:, 0:2], 0.0)
        kb.memset(XS[:, S + 2:S + 4], 0.0)
        for ci, kind in enumerate("qkv"):
            for tc in range(4):
                bank = self.PS[tc % 2]
                self.proj(bank, WC, ci * 128, tc)
                if tc % 2 == 0:
                    kb.copy(XS[:, 2 + tc * 512:2 + (tc + 1) * 512], bank[:, :])
                else:
                    kb.acopy(XS[:, 2 + tc * 512:2 + (tc + 1) * 512], bank[:, :])
            ch = ci * 3 + hp
            kb.ts(ACC[:, :], XS[:, 0:S], self.col(("conv", l, ch, 0)), 0.0, ALU.mult, ALU.add)
            for j in range(1, 5):
                kb.stt(ACC[:, :], XS[:, j:j + S], self.col(("conv", l, ch, j)), ACC[:, :], ALU.mult, ALU.add)
            if kind == "v":
                kb.activation(VTf[:, :], ACC[:, :], AF.Silu)
                continue
            kb.activation(ACC[:, :], ACC[:, :], AF.Silu)
            dst = QT if kind == "q" else KT
            for tc in range(4):
                tsl = slice(tc * 512, (tc + 1) * 512)
                kb.activation(self.SQT[:, :], ACC[:, tsl], AF.Square)
                pst = self.PS[2 + tc % 2]
                kb.mm(pst[:, :], self.BD64B, self.SQT[:, :], start=True, stop=True)
                self.rsqrt(self.RST[:, :], pst[:, :], 1.0)
                kb.stt(dst[:, tsl], ACC[:, tsl], 0.125 if kind == "q" else 1.0, self.RST[:, :], ALU.mult, ALU.mult)
        for (src, dstk) in ((KT, KTOK), (VTf, VTOK)):
            for g4 in range(4):
                bank = self.PS[4 + g4 % 2]
                for m4 in range(4):
                    m = g4 * 4 + m4
                    kb.mm(bank[:, m4 * 128:(m4 + 1) * 128], src[:, m * 128:(m + 1) * 128], self.IDB, start=True, stop=True)
                if g4 % 2 == 0:
                    kb.copy(dstk[:, g4 * 4:(g4 + 1) * 4, :], bank[:, :].rearrange("p (a b) -> p a b", a=4))
                else:
                    kb.acopy(dstk[:, g4 * 4:(g4 + 1) * 4, :], bank[:, :].rearrange("p (a b) -> p a b", a=4))
        if hp == 0:
            self.dump("QT", QT[:, :])
            self.dump("KT", KT[:, :])
            self.dump("KTOK", KTOK[:, :, :])
            self.dump("VTOK", VTOK[:, :, :])
        if self.cstop == "conv":
            return
        for hh in range(2):
            self.areset(mark)
            if self.cstop in ("t1", "t1a", "t1b", "t2", "t3", "t4", "t5") and (hp, hh) != (0, 0):
                continue
            self.c_head(l, hp, hh, QT, KT, KTOK, VTOK, ON)
        if self.cstop is not None:
            return
        self.areset(mark)
        WZ = self.aalloc([128, 8, 128], BF16)
        OUTC = self.aalloc([128, S], BF16)
        SZ = self.aalloc([128, 512], F32)
        self.load_win(l, WZ[:, :, :], (21 + hp) * 128, 128, "wz")
        for tc in range(4):
            tsl = slice(tc * 512, (tc + 1) * 512)
            pz = self.PS[tc % 2]
            self.proj(pz, WZ, 0, tc)
            kb.activation(SZ[:, :], pz[:, :], AF.Silu)
            pt = self.PS[2 + tc % 2]
            for m4 in range(4):
                m = tc * 4 + m4
                kb.mm(pt[:, m4 * 128:(m4 + 1) * 128], ON[:, m, :], self.IDB, start=True, stop=True)
            kb.stt(OUTC[:, tsl], pt[:, :], self.col(("c_out_norm", l)), SZ[:, :], ALU.mult, ALU.mult)
        self.wout(l, [5 + hp], [OUTC])

    def c_head(self, l, hp, hh, QT, KT, KTOK, VTOK, ON):
        kb = self.kb
        nc = self.nc
        h = 2 * hp + hh
        bq = hh * 64
        AT = [self.aalloc([128, 16, 128], BF16) for i in range(2)]
        U = [self.aalloc([128, 16, 64], BF16) for i in range(2)]
        KD = [self.aalloc([128, 16, 64], BF16) for i in range(2)]
        WT = self.aalloc([128, S], BF16)
        QGT = self.aalloc([128, S], BF16)
        DECB = self.aalloc([128, 2, 32], F32)
        SF = self.aalloc([128, 64], F32)
        SBb = self.aalloc([128, 64], BF16)
        VN = [self.aalloc([128, 64], BF16) for i in range(4)]
        G = 2
        NG = 16 // G
        mark_t = self.aptr
        OF = [self.aalloc([128, 16, 64], F32) for i in range(2)]
        self.aptr = mark_t
        KBGs = [self.aalloc([128, 16, 64], BF16) for i in range(2)]
        VBs = [self.aalloc([128, 16, 64], BF16) for i in range(2)]
        names = ["DN", "DT", "NN", "NT", "Q0", "Q0P", "CN", "R0", "QA", "QPA", "QB", "QPB", "R1"]
        tbs = []
        XNs, XTs = [], []
        for dr in range(2):
            t = {n: self.aalloc([128, G, 128], BF16) for n in names}
            t["T0"], t["Z"], t["TT"] = t["QA"], t["QPA"], t["QB"]
            tbs.append(t)
            XNs.append(self.aalloc([128, G, 128], F32))
            XTs.append(self.aalloc([128, G, 128], F32))
        tb = tbs[1]

        def f2(t):
            return t.rearrange("p a b -> p (a b)")

        IDB = self.IDB
        GW = G * 128
        for dr in range(2):
            r = dr * 6 + h
            kb.tt(KBGs[dr][:, :, :], KTOK[:, :, bq:bq + 64], self.tokb(2, r), ALU.mult)
            kb.tt(VBs[dr][:, :, :], VTOK[:, :, bq:bq + 64], self.tokb(3, r), ALU.mult)
            kb.tt(KD[dr][:, :, :], KTOK[:, :, bq:bq + 64], self.tokb(4, r), ALU.mult)
            pdc = self.PS[7]
            kb.mm(pdc[:, 0:32], self.SELF[0:12, r, :], self.DEC[0:12, :], start=True, stop=True)
            kb.copy(DECB[:, dr, :], pdc[:, 0:32])

        def titer(dr, gi):
            r = dr * 6 + h
            sb_ = dr * 64
            T = tbs[dr]
            XN, XT = XNs[dr], XTs[dr]
            KBG, VB = KBGs[dr], VBs[dr]
            B = [self.PS[dr * 4 + i] for i in range(4)]
            tsl = slice(gi * GW, (gi + 1) * GW)

            def mmg(bank, lt, rt):
                for m in range(G):
                    kb.mm(bank[:, m * 128:(m + 1) * 128], lt[:, m, :], rt[:, m, :] if rt is not None else IDB,
                          start=True, stop=True)

            psG, psE, psK, psQ = B[0], B[1], B[2], B[3]
            kb.mm(psG[:, 0:GW], self.SELF[0:12, r, :], self.GC[0:12, tsl], start=True, stop=True)
            kb.mm(psE[:, 0:GW], self.SELF[0:12, r, :], self.EG[0:12, tsl], start=True, stop=True)
            for m in range(G):
                tk = slice((gi * G + m) * 128, (gi * G + m + 1) * 128)
                bl = slice(m * 128, (m + 1) * 128)
                kb.mm(psK[:, bl], KT[bq:bq + 64, tk], KT[bq:bq + 64, tk], start=True, stop=True)
            for m in range(G):
                tk = slice((gi * G + m) * 128, (gi * G + m + 1) * 128)
                bl = slice(m * 128, (m + 1) * 128)
                kb.mm(psQ[:, bl], KT[bq:bq + 64, tk], QT[bq:bq + 64, tk], start=True, stop=True)
            yield
            for m in range(G):
                kb.stt(XN[:, m, :], psG[:, m * 128:(m + 1) * 128], -1.0, self.CMF[:, 4 + dr, :], ALU.mult, ALU.add)
                kb.tt(XT[:, m, :], psG[:, m * 128:(m + 1) * 128], self.CMF[:, 6 + dr, :], ALU.add)
            kb.tt(QGT[sb_:sb_ + 64, tsl], QT[bq:bq + 64, tsl], psE[bq:bq + 64, 0:GW], ALU.mult)
            for m in range(G):
                tile = gi * G + m
                kb.activation(T["DN"][:, m, :], XN[:, m, :], AF.Exp, bias=self.tok(0, tile, r))
                kb.activation(T["DT"][:, m, :], XT[:, m, :], AF.Exp, bias=self.tok(1, tile, r))
            yield
            kb.stt(f2(T["NN"]), psK[:, 0:GW], -1.0, f2(T["DN"]), ALU.mult, ALU.mult)
            kb.tt(f2(AT[dr][:, gi * G:(gi + 1) * G, :]), psQ[:, 0:GW], f2(T["DT"]), ALU.mult)
            mmg(B[0], T["NN"], None)
            yield
            kb.acopy(f2(T["NT"]), B[0][:, 0:GW])
            pe = kb.pool
            kb.tt(T["Q0P"][:, :, :], T["NN"][:, :, :], self.bcast_mid(self.CMB[:, 3, :], G), ALU.mult, eng=pe)
            kb.tt(T["CN"][:, :, :], T["NN"][:, :, :], self.bcast_mid(self.CMB[:, 8, :], G), ALU.mult, eng=pe)
            yield
            kb.tt(T["Q0"][:, :, :], T["NT"][:, :, :], self.bcast_mid(self.CMB[:, 3, :], G), ALU.mult, eng=pe)
            kb.tt(T["R0"][:, :, :], T["Q0"][:, :, :], self.bcast_mid(self.CMB[:, 0, :], G), ALU.add, eng=pe)
            yield
            Q, QP, R = T["Q0"], T["Q0P"], T["R0"]
            alt = [(T["QA"], T["QPA"]), (T["QB"], T["QPB"])]
            for k in range(1, 5):
                Qn, QPn = alt[(k - 1) % 2]
                Rn = T["R1"] if k % 2 == 1 else T["R0"]
                pA, pB, pC = B[1], B[2], B[3]
                if k < 4:
                    mmg(pA, QP, Q)
                mmg(pB, Q, QP)
                yield
                if k < 4:
                    kb.acopy(f2(Qn), pA[:, 0:GW])
                if k % 2 == 0:
                    kb.acopy(f2(QPn), pB[:, 0:GW])
                else:
                    kb.copy(f2(QPn), pB[:, 0:GW])
                yield
                mmg(pC, QPn, R)
                yield
                kb.tt(f2(Rn), pC[:, 0:GW], f2(R), ALU.add)
                yield
                Q, QP, R = Qn, QPn, Rn
            pD, pE, pF = B[0], B[1], B[2]
            mmg(pD, R, None)
            mmg(pE, T["CN"], R)
            yield
            kb.acopy(f2(T["T0"]), pD[:, 0:GW])
            kb.copy(f2(T["Z"]), pE[:, 0:GW])
            yield
            mmg(pF, T["T0"], T["Z"])
            yield
            kb.tt(f2(T["TT"]), pF[:, 0:GW], f2(R), ALU.add)
            yield
            psU, psW = B[3], B[0]
            for m in range(G):
                tile = gi * G + m
                kb.mm(psU[:, m * 64:(m + 1) * 64], T["TT"][:, m, :], VB[:, tile, :], start=True, stop=True)
            for m in range(G):
                tile = gi * G + m
                kb.mm(psW[0:64, m * 128:(m + 1) * 128], KBG[:, tile, :], T["TT"][:, m, :], start=True, stop=True)
            yield
            kb.acopy(f2(U[dr][:, gi * G:(gi + 1) * G, :]), psU[:, 0:G * 64])
            kb.copy(WT[sb_:sb_ + 64, tsl], psW[0:64, 0:GW])

        for gi in range(NG):
            gens = [titer(0, gi), titer(1, gi)]
            alive = [True, True]
            while any(alive):
                for i in range(2):
                    if alive[i]:
                        try:
                            next(gens[i])
                        except StopIteration:
                            alive[i] = False
        if hp == 0 and hh == 0:
            self.dump("AT0", AT[0][:, :, :])
            self.dump("AT1", AT[1][:, :, :])
            self.dump("U0", U[0][:, :, :])
            self.dump("U1", U[1][:, :, :])
            self.dump("WT", WT[:, :])
            self.dump("QGT", QGT[:, :])
            self.dump("KD0", KD[0][:, :, :])
            self.dump("DECB", DECB[:, :, :])
            self.dump("TT", tb["TT"][:, :, :])
            self.dump("NN", tb["NN"][:, :, :])
        if self.cstop == "tstage":
            return
        kb.memset(SF[:, :], 0.0)
        kb.memset(SBb[:, :], 0.0)
        for s in range(32):
            for dr in range(2):
                c = s if dr == 0 else 31 - s
                m = c // 2
                pb = (c % 2) * 64
                sb_ = dr * 64
                bW, bO, bS = self.PS[dr * 4], self.PS[dr * 4 + 1], self.PS[dr * 4 + 2]
                vn = VN[dr * 2 + s % 2]
                csl = slice(c * 64, (c + 1) * 64)
                kb.mm(bW[0:64, 0:64], WT[sb_:sb_ + 64, csl], SBb[sb_:sb_ + 64, :], start=True, stop=True)
                kb.tt(vn[pb:pb + 64, :], U[dr][pb:pb + 64, m, :], bW[0:64, 0:64], ALU.subtract)
                kb.mm(bO[0:64, 0:64], QGT[sb_:sb_ + 64, csl], SBb[sb_:sb_ + 64, :], start=True, stop=False, inc=False)
                kb.mm(bO[0:64, 0:64], AT[dr][pb:pb + 64, m, pb:pb + 64], vn[pb:pb + 64, :], start=False, stop=True)
                kb.acopy(OF[dr][pb:pb + 64, m, :], bO[0:64, 0:64])
                kb.mm(bS[0:64, 0:64], KD[dr][pb:pb + 64, m, :], vn[pb:pb + 64, :], start=True, stop=True)
                kb.stt(SF[sb_:sb_ + 64, :], SF[sb_:sb_ + 64, :], DECB[sb_:sb_ + 64, dr, c:c + 1], bS[0:64, 0:64],
                       ALU.mult, ALU.add)
                kb.acopy(SBb[sb_:sb_ + 64, :], SF[sb_:sb_ + 64, :])
        if hp == 0 and hh == 0:
            self.dump("OF0", OF[0][:, :, :])
            self.dump("OF1", OF[1][:, :, :])
        if self.cstop == "scan":
            return
        OS = OF[0]
        kb.tt(OS[:, :, :], OF[0][:, :, :], OF[1][:, :, :], ALU.add)
        kb.tt(OF[1][:, :, :], OS[:, :, :], OS[:, :, :], ALU.mult)
        MS = self.aalloc([128, 16], F32)
        kb.op(kb.dve, lambda: nc.vector.tensor_reduce(out=MS[:, :], in_=OF[1][:, :, :], axis=AX.X, op=ALU.add),
              reads=[OF[1][:, :, :]], writes=[MS[:, :]])
        self.rsqrt(MS[:, :], MS[:, :], 1.0 / 64)
        kb.tt(ON[:, :, bq:bq + 64], OS[:, :, :], self.bcast_last(MS[:, :].rearrange("p (a o) -> p a o", o=1), 64), ALU.mult)


def prepare(inputs):
    cols = make_cols(inputs)
    shared = {"cols": cols.array()}
    for nm in ("ffn1", "ffn2"):
        shared[nm + "_wg"] = np.ascontiguousarray(inputs[nm + "_w_gate"], dtype=np.float32)
        shared[nm + "_wu"] = np.ascontiguousarray(inputs[nm + "_w_up"], dtype=np.float32)
        shared[nm + "_wd"] = np.ascontiguousarray(inputs[nm + "_w_down"], dtype=np.float32)
    shared["win"] = np.ascontiguousarray(np.asarray(inputs["w_in"], np.float32)[:, :, win_perm()])
    shared["wout"] = np.ascontiguousarray(inputs["w_out"], dtype=np.float32)
    shared["relb"] = np.ascontiguousarray(inputs["rel_bias"], dtype=np.float32)
    shared.update(make_consts())
    return cols, shared


def run(inputs, cores=8, stop_after=None, parts="ABC", debug=False, cstop=None, ret_all=False):
    inputs = {k: np.asarray(v) for k, v in inputs.items()}
    cols, shared = prepare(inputs)
    prog = Prog(len(cols.cols), cols.idx, stop_after=stop_after, parts=parts)
    prog.debug = debug
    prog.cstop = cstop
    nc = prog.build()
    x = inputs["x"]
    in_maps = []
    for c in range(cores):
        m = dict(shared)
        m["xT"] = np.ascontiguousarray(x[c].T)
        in_maps.append(m)
    res = run_bass_kernel_spmd(nc, in_maps, core_ids=list(range(cores)))
    out = np.stack([np.ascontiguousarray(res.results[c]["yT"].T) for c in range(cores)], axis=0)
    if ret_all:
        return out.astype(np.float32), res.results
    return out.astype(np.float32)


def kernel(**inputs):
    return run(inputs, cores=8)
```
